# Optimizing a Trainium2 kernel written in Bass

```python
import jax, jax.numpy as jnp
from jax import lax
import numpy as np

D_MODEL = 2048
BATCH = 1
SEQ = 8192
DEPTH = 2
DEC_BATCH = 2
DEC_SEQ = 16384
PAST_LEN = 128

N_MIXERS = 2
EXPAND = 2
D_INNER = EXPAND * D_MODEL
CHUNK_A = 128
N_GROUPS_A = 16
GROUP_DIM_A = D_INNER // N_GROUPS_A
GLA_HEADS = 4
GLA_KEY_DIM = D_MODEL // 2
GLA_HEAD_K = GLA_KEY_DIM // GLA_HEADS
GLA_HEAD_V = D_INNER // GLA_HEADS
GATE_RANK = 16
GATE_TAU = 16.0
CHUNK_B = 64
N_A_LAYERS = (DEPTH + 1) // 2
N_B_LAYERS = DEPTH // 2
DEEPNORM_ALPHA = (2 * DEPTH) ** 0.25
DEEPNORM_BETA = (8 * DEPTH) ** -0.25
LN_EPS = 1e-5
RMS_EPS = 1e-6
B_IN_COLS = 2 * GLA_KEY_DIM + 2 * D_INNER + 2 * GATE_RANK

kernel_name = "hybrid_bidir_gmlp_gla_deepnorm"


def layer_norm(x, g, b):
    xf = x.astype(jnp.float32)
    mu = jnp.mean(xf, axis=-1, keepdims=True)
    xc = xf - mu
    var = jnp.mean(xc * xc, axis=-1, keepdims=True)
    return (xc * lax.rsqrt(var + LN_EPS) * g.astype(jnp.float32) + b.astype(jnp.float32)).astype(x.dtype)


def mixer_a(x, w_in, ln_v_g, ln_v_b, w_s, b_s, w_out):
    B, T, _ = x.shape
    h = x @ w_in
    u, v, z = jnp.split(h, 3, axis=-1)
    u = jax.nn.gelu(u)
    v = layer_norm(jax.nn.gelu(v), ln_v_g, ln_v_b)
    vc = v.reshape(B, T // CHUNK_A, CHUNK_A, N_GROUPS_A, GROUP_DIM_A)
    s = jnp.einsum('gts,bnsgc->bntgc', w_s, vc) + b_s.T[None, None, :, :, None]
    s = s.reshape(B, T, D_INNER)
    y = u * s * jax.nn.silu(z)
    return y @ w_out


def gla_chunked(q, k, v, g, exclusive):
    B, T, H, dk = q.shape
    dv = v.shape[-1]
    N = T // CHUNK_B

    def to_chunks(a):
        return a.astype(jnp.float32).reshape(B, N, CHUNK_B, H, a.shape[-1]).transpose(1, 0, 3, 2, 4)

    qc, kc, vc, gc = to_chunks(q), to_chunks(k), to_chunks(v), to_chunks(g)
    G = jnp.cumsum(gc, axis=3)
    mask = jnp.tril(jnp.ones((CHUNK_B, CHUNK_B), dtype=bool), k=-1 if exclusive else 0)
    mid = CHUNK_B // 2

    def step(S, inp):
        qn, kn, vn, Gn = inp
        Gmid = Gn[:, :, mid:mid + 1]
        Glast = Gn[:, :, -1:]
        a = jnp.einsum('bhtd,bhsd->bhts', qn * jnp.exp(Gn - Gmid), kn * jnp.exp(Gmid - Gn))
        a = jnp.where(mask, a, 0.0)
        o = jnp.einsum('bhts,bhse->bhte', a, vn) + jnp.einsum('bhtd,bhde->bhte', qn * jnp.exp(Gn), S)
        S = jnp.exp(Glast)[:, :, 0, :, None] * S + jnp.einsum('bhsd,bhse->bhde', kn * jnp.exp(Glast - Gn), vn)
        return S, o

    S0 = jnp.zeros((B, H, dk, dv), jnp.float32)
    _, o = lax.scan(step, S0, (qc, kc, vc, G))
    return o.transpose(1, 0, 3, 2, 4).reshape(B, T, H, dv)


def mixer_b(x, w_in, w_g2, b_g, gn_g, w_out):
    B, T, _ = x.shape
    h = x @ w_in
    K, E = GLA_KEY_DIM, D_INNER
    q, k, v, z, gl = jnp.split(h, [K, 2 * K, 2 * K + E, 2 * K + 2 * E], axis=-1)
    q = q.reshape(B, T, GLA_HEADS, GLA_HEAD_K) * (GLA_HEAD_K ** -0.5)
    k = k.reshape(B, T, GLA_HEADS, GLA_HEAD_K)
    v = v.reshape(B, T, GLA_HEADS, GLA_HEAD_V)
    gl = gl.reshape(B, T, 2, GATE_RANK).astype(jnp.float32)
    glog = jax.nn.log_sigmoid(jnp.einsum('btdr,drk->btdk', gl, w_g2.astype(jnp.float32))
                              + b_g.astype(jnp.float32)) / GATE_TAU
    g_f = glog[:, :, 0].reshape(B, T, GLA_HEADS, GLA_HEAD_K)
    g_b = glog[:, :, 1].reshape(B, T, GLA_HEADS, GLA_HEAD_K)
    o_f = gla_chunked(q, k, v, g_f, exclusive=False)
    flip = lambda a: jnp.flip(a, axis=1)
    o_b = flip(gla_chunked(flip(q), flip(k), flip(v), flip(g_b), exclusive=True))
    o = o_f + o_b
    o = o * lax.rsqrt(jnp.mean(o * o, axis=-1, keepdims=True) + RMS_EPS)
    o = o * gn_g.astype(jnp.float32).reshape(GLA_HEADS, GLA_HEAD_V)
    y = o.reshape(B, T, E).astype(x.dtype) * jax.nn.silu(z)
    return y @ w_out


def trunk(x, w_in_a, ln_v_g_a, ln_v_b_a, w_s_a, b_s_a, w_out_a,
          w_in_b, w_g2_b, b_g_b, gn_g_b, w_out_b, ln_g, ln_b):
    for i in range(DEPTH):
        j = i // N_MIXERS
        if i % N_MIXERS == 0:
            f = mixer_a(x, w_in_a[j], ln_v_g_a[j], ln_v_b_a[j], w_s_a[j], b_s_a[j], w_out_a[j])
        else:
            f = mixer_b(x, w_in_b[j], w_g2_b[j], b_g_b[j], gn_g_b[j], w_out_b[j])
        x = layer_norm(DEEPNORM_ALPHA * x + f, ln_g[i], ln_b[i])
    return x


def setup_inputs(seed: int = 0) -> dict:
    key = jax.random.key(seed)
    ks = jax.random.split(key, 16)
    nrm = jax.random.normal
    f32 = jnp.float32
    return {
        "x_prompt": nrm(ks[0], (BATCH, SEQ, D_MODEL), f32),
        "x_sample": nrm(ks[1], (DEC_BATCH, DEC_SEQ, D_MODEL), f32),
        "w_in_a": nrm(ks[2], (N_A_LAYERS, D_MODEL, 3 * D_INNER), f32) * D_MODEL ** -0.5,
        "ln_v_g_a": 1.0 + 0.1 * nrm(ks[3], (N_A_LAYERS, D_INNER), f32),
        "ln_v_b_a": 0.02 * nrm(ks[4], (N_A_LAYERS, D_INNER), f32),
        "w_s_a": nrm(ks[5], (N_A_LAYERS, N_GROUPS_A, CHUNK_A, CHUNK_A), f32) * CHUNK_A ** -0.5,
        "b_s_a": 1.0 + 0.1 * nrm(ks[6], (N_A_LAYERS, N_GROUPS_A, CHUNK_A), f32),
        "w_out_a": nrm(ks[7], (N_A_LAYERS, D_INNER, D_MODEL), f32) * (D_INNER ** -0.5 * DEEPNORM_BETA),
        "w_in_b": nrm(ks[8], (N_B_LAYERS, D_MODEL, B_IN_COLS), f32) * D_MODEL ** -0.5,
        "w_g2_b": nrm(ks[9], (N_B_LAYERS, 2, GATE_RANK, GLA_KEY_DIM), f32) * GATE_RANK ** -0.5,
        "b_g_b": 0.1 * nrm(ks[10], (N_B_LAYERS, 2, GLA_KEY_DIM), f32),
        "gn_g_b": 1.0 + 0.1 * nrm(ks[11], (N_B_LAYERS, D_INNER), f32),
        "w_out_b": nrm(ks[12], (N_B_LAYERS, D_INNER, D_MODEL), f32) * (D_INNER ** -0.5 * DEEPNORM_BETA),
        "ln_g": 1.0 + 0.1 * nrm(ks[13], (DEPTH, D_MODEL), f32),
        "ln_b": 0.02 * nrm(ks[14], (DEPTH, D_MODEL), f32),
    }


def reference(x_prompt, x_sample, w_in_a, ln_v_g_a, ln_v_b_a, w_s_a, b_s_a, w_out_a,
              w_in_b, w_g2_b, b_g_b, gn_g_b, w_out_b, ln_g, ln_b):
    y_prompt = trunk(x_prompt, w_in_a, ln_v_g_a, ln_v_b_a, w_s_a, b_s_a, w_out_a,
                     w_in_b, w_g2_b, b_g_b, gn_g_b, w_out_b, ln_g, ln_b)
    y_sample = trunk(x_sample, w_in_a, ln_v_g_a, ln_v_b_a, w_s_a, b_s_a, w_out_a,
                     w_in_b, w_g2_b, b_g_b, gn_g_b, w_out_b, ln_g, ln_b)
    return (y_prompt, y_sample)
```

```python
import os
import numpy as np
import ml_dtypes
KSKIP = os.environ.get('KSKIP', '')
from contextlib import ExitStack
import concourse.bass as bass
import concourse.mybir as mybir
from concourse.bass_utils import run_bass_kernel_spmd

F32 = mybir.dt.float32
BF16 = mybir.dt.bfloat16
AF = mybir.ActivationFunctionType
ALU = mybir.AluOpType

D = 2048
DI = 4096
ALPHA = 4.0 ** 0.25
LN_EPS = 1e-5
RMS_EPS = 1e-6
NCORES = 8
OWN = 5120
HALO = 512


class Buf:
    __slots__ = ("name", "w", "r", "dsem", "dcnt")

    def __init__(self, name):
        self.name = name
        self.w = None
        self.r = {}
        self.dsem = None
        self.dcnt = 0


class Sched:
    CE = ("pe", "act", "dve", "pool")

    def __init__(self, nc, es):
        self.nc = nc
        self.es = es
        self.q = {e: [] for e in ("pe", "act", "dve", "pool", "sp")}
        self.sems = {e: es.enter_context(nc.semaphore("sem_" + e)) for e in self.CE}
        self.cnt = {e: 0 for e in self.CE}
        self.seen = {e: {} for e in self.q}
        self.dbufs = []

    def _semof(self, k):
        return self.sems[k] if isinstance(k, str) else k.dsem

    def _need(self, e, toks):
        best = {}
        for t in toks:
            if t is None:
                continue
            k, v = t
            if k == e and e == "pe":
                continue
            if self.seen[e].get(k, 0) >= v:
                continue
            if best.get(k, 0) < v:
                best[k] = v
        for k, v in best.items():
            self.seen[e][k] = v
            sem = self._semof(k)
            self.q[e].append(lambda E, sem=sem, v=v: E.wait_ge(sem, v))

    def _deps(self, reads, writes):
        toks = []
        for b in reads:
            toks.append(b.w)
        for b in writes:
            toks.append(b.w)
            toks.extend(b.r.items())
        return toks

    def _update(self, tok, reads, writes):
        for b in writes:
            b.w = tok
            b.r = {}
        for b in reads:
            if b not in writes:
                k, v = tok
                if b.r.get(k, 0) < v:
                    b.r[k] = v

    mute = False

    def op(self, e, fn, reads=(), writes=(), inc=True):
        if self.mute:
            return
        self._need(e, self._deps(reads, writes))
        if inc:
            self.cnt[e] += 1
            v = self.cnt[e]
            sem = self.sems[e]
            self.q[e].append(lambda E, fn=fn, sem=sem: fn(E).then_inc(sem, 1))
            tok = (e, v)
        else:
            self.q[e].append(lambda E, fn=fn: fn(E))
            tok = (e, self.cnt[e] + 1)
        self._update(tok, reads, writes)

    def dma(self, out_ap, in_ap, reads, writes, sb, q="sp"):
        if self.mute:
            return
        if sb.dsem is None:
            sb.dsem = self.es.enter_context(self.nc.semaphore("d_" + sb.name))
            self.dbufs.append(sb)
        toks = self._deps(reads, writes)
        if sb.dcnt > 0:
            toks.append((sb, sb.dcnt))
        self._need(q, toks)
        sb.dcnt += 16
        v = sb.dcnt
        sem = sb.dsem
        self.q[q].append(lambda E, o=out_ap, i=in_ap, sem=sem: E.dma_start(out=o, in_=i).then_inc(sem, 16))
        self._update((sb, v), reads, writes)

    def barrier(self):
        toks = [(e, self.cnt[e]) for e in self.CE if self.cnt[e] > 0]
        toks += [(b, b.dcnt) for b in self.dbufs if b.dcnt > 0]
        for e in self.q:
            self._need(e, [t for t in toks if t[0] != e or e != "pe"])

    def finish(self):
        with self.nc.Block() as block:
            q = self.q

            @block.sync
            def _(E):
                for t in q["sp"]:
                    t(E)

            @block.tensor
            def _(E):
                for t in q["pe"]:
                    t(E)

            @block.scalar
            def _(E):
                for t in q["act"]:
                    t(E)

            @block.vector
            def _(E):
                for t in q["dve"]:
                    t(E)

            @block.gpsimd
            def _(E):
                for t in q["pool"]:
                    t(E)


def build(NT, T0, T1, stage=9):
    NW = NT * 512
    NCH = NT * 4
    NOWN = (T1 - T0) * 512
    nc = bass.Bass("TRN2", target_bir_lowering=False)

    def din(name, shape, dt=F32):
        return nc.dram_tensor(name, list(shape), dt, kind="ExternalInput").ap()

    def dscr(name, shape, dt):
        return nc.dram_tensor(name, list(shape), dt, kind="Internal").ap()

    xw = din("xw", [NW, D])
    w_in_a = din("w_in_a", [D, 3 * DI])
    wsT_in = din("wsT", [128, 2048])
    b_s = din("b_s_a", [1, 16 * 128])
    w_out_a = din("w_out_a", [DI, D])
    w_in_b = din("w_in_b", [D, 10272])
    w_g2 = din("w_g2_b", [2, 16, 1024])
    b_g = din("b_g_b", [1, 2048])
    w_out_b = din("w_out_b", [DI, D])
    ln_g = din("ln_g", [2, D])
    ln_b = din("ln_b", [2, D])
    CW = 7 * 128 + 2 * NCH + 128 + 256
    cst = din("cst", [128, CW])
    yout = nc.dram_tensor("y", [NOWN, D], F32, kind="ExternalOutput").ap()

    wA_in_s = dscr("wA_in_s", [24, 128, 16 * 512], BF16)
    wA_out_s = dscr("wA_out_s", [8, 128, 16 * 512], BF16)
    wB_in_s = dscr("wB_in_s", [20, 128, 16 * 512], BF16)
    wB_out_s = dscr("wB_out_s", [8, 128, 16 * 512], BF16)
    x1h_s = dscr("x1h_s", [NW, D], F32)
    qT_s = dscr("qT_s", [NCH, 128, 8, 128], BF16)
    kT_s = dscr("kT_s", [NCH, 128, 8, 128], BF16)
    zT_s = dscr("zT_s", [NCH, 128, 32, 128], BF16)
    ktok_s = dscr("ktok_s", [NW, 1024], BF16)
    v_s = dscr("v_s", [NW, DI], BF16)
    sp_s = dscr("sp_s", [NW, 4096], BF16)
    of_s = dscr("of_s", [NW, DI], F32)
    yT_s = dscr("yT_s", [NCH, 128, 32, 128], BF16)

    es_all = ExitStack()
    with es_all:
        S = Sched(nc, es_all)
        bufs = {}

        def B(name):
            if name not in bufs:
                bufs[name] = Buf(name.replace("/", "_").replace(":", "_"))
            return bufs[name]

        psum = [es_all.enter_context(nc.psum_tensor(f"ps{i}", [128, 512], F32)) for i in range(8)]
        psb = [Buf(f"ps{i}") for i in range(8)]
        pstate = {"i": 0}

        def nps():
            i = pstate["i"]
            pstate["i"] = (i + 1) % 8
            return psum[i], psb[i]

        E = es_all.enter_context
        cst_t = E(nc.sbuf_tensor("cst_t", [128, CW], F32))
        tribf = E(nc.sbuf_tensor("tribf", [128, 6, 128], BF16))
        identb = E(nc.sbuf_tensor("identb", [128, 128], BF16))
        cols = cst_t[:, 896 + 2 * NCH:896 + 2 * NCH + 128]
        Bc = B("consts")
        ident = cst_t[:, 0:128]
        tri = [tribf[:, 0, :], tribf[:, 2, :]]
        triR = [tribf[:, 1, :], tribf[:, 3, :]]
        maskT = [cst_t[:, 640:768], cst_t[:, 768:896]]
        keep = [cst_t[:, 896:896 + NCH], cst_t[:, 896 + NCH:896 + 2 * NCH]]

        epsc = E(nc.sbuf_tensor("epsc", [128, 2], F32))
        S.op("dve", lambda e: e.memset(epsc[:, 0:1], LN_EPS), [], [Bc])
        S.op("dve", lambda e: e.memset(epsc[:, 1:2], RMS_EPS), [], [Bc])
        S.dma(cst_t[:], cst[:, :], [], [Bc], Bc)
        S.op("dve", lambda e: e.tensor_copy(out=identb[:], in_=ident), [Bc], [Bc])
        S.op("dve", lambda e: e.tensor_copy(out=tribf[:, 0:4, :].rearrange("p a b -> p (a b)"), in_=cst_t[:, 128:640]), [Bc], [Bc])
        S.op("dve", lambda e: e.tensor_copy(out=tribf[:, 4:6, :].rearrange("p a b -> p (a b)"), in_=cst_t[:, CW - 256:CW]), [Bc], [Bc])
        triC = [tribf[:, 4, :], tribf[:, 5, :]]

        lnvg = cols[:, 0:32]
        lnvb = cols[:, 32:64]
        gncol = cols[:, 64:96]
        l0g = cols[:, 96:112]
        l0b = cols[:, 112:128]

        with ExitStack() as es0:
            S.mute = ('b' in KSKIP)
            E0 = es0.enter_context
            st32 = [E0(nc.sbuf_tensor(f"st32_{i}", [128, 8, 512], F32)) for i in range(6)]
            st16 = [E0(nc.sbuf_tensor(f"st16_{i}", [128, 8 * 512], BF16)) for i in range(6)]
            jobs = []
            wa = w_in_a.rearrange("(k p) n -> p k n", p=128)
            for g in range(24):
                for h in range(2):
                    jobs.append((wa[:, h * 8:(h + 1) * 8, g * 512:(g + 1) * 512], wA_in_s[g, :, h * 4096:(h + 1) * 4096]))
            wb = w_in_b.rearrange("(k p) n -> p k n", p=128)
            for g in range(20):
                for h in range(2):
                    jobs.append((wb[:, h * 8:(h + 1) * 8, g * 512:(g + 1) * 512], wB_in_s[g, :, h * 4096:(h + 1) * 4096]))
            for (wsrc, wdst) in ((w_out_a, wA_out_s), (w_out_b, wB_out_s)):
                wo = wsrc.rearrange("(j p) n -> p j n", p=128)
                for n in range(4):
                    for hf in range(2):
                        for qq in range(2):
                            j0 = hf * 16 + qq * 8
                            jobs.append((wo[:, j0:j0 + 8, n * 512:(n + 1) * 512],
                                         wdst[n * 2 + hf, :, qq * 4096:(qq + 1) * 4096]))
            engs = ["act", "dve"]
            if stage < 1:
                jobs = []
            NSL = 6

            def p0_load(idx):
                if idx < len(jobs):
                    s_ = idx % NSL
                    S.dma(st32[s_][:], jobs[idx][0], [], [B(f"st32_{s_}")], B(f"st32_{s_}"))

            for idx in range(NSL):
                p0_load(idx)
            for idx, (src, dst) in enumerate(jobs):
                s = idx % NSL
                b32 = B(f"st32_{s}")
                b16 = B(f"st16_{s}")
                en = engs[idx % 2]
                src_flat = st32[s][:].rearrange("p k c -> p (k c)")
                if en == "act":
                    S.op("act", lambda e, s=s, sf=src_flat: e.activation(out=st16[s][:], in_=sf, func=AF.Copy), [b32], [b16])
                else:
                    S.op(en, lambda e, s=s, sf=src_flat: e.tensor_copy(out=st16[s][:], in_=sf), [b32], [b16])
                S.dma(dst, st16[s][:], [b16], [B("wscr")], b16)
                p0_load(idx + NSL)
            S.mute = False
            S.barrier()

        wgl = E(nc.sbuf_tensor("wgl", [128, 16, 32], BF16))
        W2h = E(nc.sbuf_tensor("W2h", [33, 2048], BF16))
        W2l = E(nc.sbuf_tensor("W2l", [33, 2048], BF16))
        glh = E(nc.sbuf_tensor("glh", [33, 512], BF16))
        gll = E(nc.sbuf_tensor("gll", [33, 512], BF16))
        with ExitStack() as es0:
            S.mute = ('c' in KSKIP)
            E0 = es0.enter_context
            wgl32 = E0(nc.sbuf_tensor("wgl32", [128, 16, 32], F32))
            W2 = E0(nc.sbuf_tensor("W2", [33, 2048], F32))
            Bw = B("wgl32")
            S.dma(wgl32[:], w_in_b.rearrange("(k p) n -> p k n", p=128)[:, :, 10240:10272], [], [Bw], Bw)
            S.op("dve", lambda e: e.tensor_copy(out=wgl[:], in_=wgl32[:]), [Bw], [Bc])
            S.op("dve", lambda e: e.memset(W2[:], 0.0), [], [Bc])
            S.op("dve", lambda e: e.memset(glh[:], 1.0), [], [B("glT")])
            S.op("dve", lambda e: e.memset(gll[:], 0.0), [], [B("glT")])
            S.dma(W2[0:16, 0:1024], w_g2[0, :, :], [], [Bc], Bc)
            S.dma(W2[16:32, 1024:2048], w_g2[1, :, :], [], [Bc], Bc)
            S.dma(W2[32:33, :], b_g[:, :], [], [Bc], Bc)
            S.op("dve", lambda e: e.tensor_copy(out=W2h[:], in_=W2[:]), [Bc], [B("W2h")])
            S.op("dve", lambda e: e.tensor_tensor(out=W2l[:], in0=W2[:], in1=W2h[:], op=ALU.subtract), [Bc, B("W2h")], [B("W2l")])
            S.mute = False
            S.barrier()

        with ExitStack() as es1:
            E1 = es1.enter_context
            WsTb = E1(nc.sbuf_tensor("WsTb", [128, 16, 128], BF16))
            Bias = E1(nc.sbuf_tensor("Bias", [128, 32, 128], F32))
            with ExitStack() as es0:
                S.mute = ('d' in KSKIP)
                E0 = es0.enter_context
                WsTf = E0(nc.sbuf_tensor("WsTf", [128, 16, 128], F32))
                WsTl = E0(nc.sbuf_tensor("WsTl", [128, 16, 128], BF16))
                onesb = E0(nc.sbuf_tensor("onesb", [128, 128], BF16))
                Rb = E0(nc.sbuf_tensor("Rb", [128, 16, 128], F32))
                bsb = E0(nc.sbuf_tensor("bsb", [128, 16 * 128], F32))
                Bs = B("spsetup")
                S.dma(WsTf[:].rearrange("p a b -> p (a b)"), wsT_in[:, :], [], [Bs], Bs)
                S.dma(bsb[:], b_s[0:1, :].broadcast_to([128, 2048]), [], [B("bsb")], B("bsb"))
                S.op("dve", lambda e: e.memset(onesb[:], 1.0), [], [B("onesb")])
                S.op("dve", lambda e: e.tensor_copy(out=WsTb[:], in_=WsTf[:]), [Bs], [B("WsTb")])
                S.op("dve", lambda e: e.tensor_tensor(out=WsTl[:], in0=WsTf[:], in1=WsTb[:], op=ALU.subtract), [Bs, B("WsTb")], [B("WsTl")])
                for g4 in range(4):
                    pt, pb_ = nps()
                    for gg in range(4):
                        g = g4 * 4 + gg
                        S.op("pe", lambda e, g=g, gg=gg, pt=pt: e.matmul(pt[:, gg * 128:(gg + 1) * 128], lhsT=onesb[:], rhs=WsTb[:, g, :],
                                                                  start=True, stop=False), [B("onesb"), B("WsTb"), B("WsTl")], [pb_], inc=False)
                        S.op("pe", lambda e, g=g, gg=gg, pt=pt: e.matmul(pt[:, gg * 128:(gg + 1) * 128], lhsT=onesb[:], rhs=WsTl[:, g, :],
                                                                  start=False, stop=True), [B("onesb"), B("WsTb"), B("WsTl")], [pb_], inc=(gg == 3))
                    S.op("dve", lambda e, g4=g4, pt=pt: e.tensor_copy(out=Rb[:, g4 * 4:(g4 + 1) * 4, :].rearrange("p a b -> p (a b)"), in_=pt[:]),
                         [pb_], [B("Rb")])
                for j in range(32 if 'w' not in KSKIP else 0):
                    g = j // 2
                    S.op("dve", lambda e, j=j, g=g: e.scalar_tensor_tensor(out=Bias[:, j, :], in0=Rb[:, g, :], scalar=lnvb[:, j:j + 1],
                                                                         in1=bsb[:, g * 128:(g + 1) * 128], op0=ALU.mult, op1=ALU.add),
                         [B("Rb"), B("bsb"), Bc], [B("Bias")])
                S.mute = False
                S.barrier()

            bigA = E1(nc.sbuf_tensor("bigA", [128, 4, DI], BF16))
            xT = E1(nc.sbuf_tensor("xT", [128, 16, 512], BF16))
            pbuf = E1(nc.sbuf_tensor("pbuf", [128, 32, 512], BF16))
            wsl = [E1(nc.sbuf_tensor(f"wsl{i}", [128, 16, 512], BF16)) for i in range(2)]
            xin = [E1(nc.sbuf_tensor(f"xin{i}", [128, D], F32)) for i in range(2)]
            xb = [E1(nc.sbuf_tensor(f"xb{i}", [128, D], BF16)) for i in range(2)]
            tmpb = [E1(nc.sbuf_tensor(f"tmpb{i}", [128, 512], BF16)) for i in range(2)]
            tmpf = [E1(nc.sbuf_tensor(f"tmpf{i}", [128, 512], F32)) for i in range(2)]
            stg = [E1(nc.sbuf_tensor(f"stg{i}", [128, 4, 512], BF16)) for i in range(2)]
            stf = [E1(nc.sbuf_tensor(f"stf{i}", [128, 512], F32)) for i in range(2)]
            sps = [E1(nc.sbuf_tensor(f"sps{i}", [128, 2, 512], BF16)) for i in range(2)]
            stats = E1(nc.sbuf_tensor("stats", [128, 4, 8, 6], F32))
            mv = E1(nc.sbuf_tensor("mv", [128, 4, 2], F32))
            rstd = E1(nc.sbuf_tensor("rstd", [128, 4], F32))

            wlist = []
            for n in range(8):
                wlist.append(wA_in_s[8 + n])
            for g in range(8):
                wlist.append(wA_in_s[g])
                wlist.append(wA_in_s[16 + g])
            for n in range(8):
                wlist.append(wA_out_s[n])
            wseq = []
            for ti in range(NT):
                wseq.extend(wlist)
                is_own = T0 <= ti < T1
                for g in list(range(0, 4)) + list(range(12, 20)):
                    if is_own or g in (2, 3):
                        wseq.append(wB_in_s[g])
                for g in range(2, 12):
                    wseq.append(wB_in_s[g])
            wstate = {"issued": 0, "used": 0}
            total_w = len(wseq)

            def w_issue():
                k = wstate["issued"]
                if k >= total_w:
                    return
                s = k % 2
                bw = B(f"wsl{s}")
                S.dma(wsl[s][:].rearrange("p k c -> p (k c)"), wseq[k], [B("wscr")], [bw], bw)
                wstate["issued"] = k + 1

            def w_next():
                k = wstate["used"]
                while wstate["issued"] <= k:
                    w_issue()
                wstate["used"] = k + 1
                return wsl[k % 2], B(f"wsl{k % 2}")

            def w_done():
                w_issue()

            if stage >= 2:
                w_issue()
                w_issue()
            rr = {"i": 0}

            def alt(lst):
                rr["i"] += 1
                return rr["i"] % len(lst)

            def transpose_block(src_bf, srcB, b, affine):
                for half in range(2):
                    pt, pb_ = nps()
                    ptb = pt[:].bitcast(BF16)
                    for kk in range(8):
                        k = half * 8 + kk
                        S.op("pe", lambda e, k=k, kk=kk, ptb=ptb: e.transpose(out=ptb[:, kk * 128:(kk + 1) * 128], in_=src_bf[:, k * 128:(k + 1) * 128],
                                                                               identity=identb[:]), [srcB, Bc], [pb_], inc=(kk == 7))
                    if not affine:
                        S.op("act", lambda e, half=half, ptb=ptb: e.activation(out=xT[:, half * 8:(half + 1) * 8, b * 128:(b + 1) * 128],
                                                                             in_=ptb.rearrange("p (a c) -> p a c", a=8), func=AF.Copy),
                             [pb_], [B(f"xT{b}")])
                    else:
                        for kk in range(8):
                            k = half * 8 + kk
                            if kk % 2 == 0:
                                S.op("dve", lambda e, k=k, kk=kk, ptb=ptb: e.tensor_scalar(out=xT[:, k, b * 128:(b + 1) * 128], in0=ptb[:, kk * 128:(kk + 1) * 128],
                                                                                    scalar1=l0g[:, k:k + 1], scalar2=l0b[:, k:k + 1], op0=ALU.mult, op1=ALU.add),
                                     [pb_, Bc], [B(f"xT{b}")])
                            else:
                                S.op("act", lambda e, k=k, kk=kk, ptb=ptb: e.activation(out=xT[:, k, b * 128:(b + 1) * 128], in_=ptb[:, kk * 128:(kk + 1) * 128],
                                                                                 func=AF.Identity, bias=l0b[:, k:k + 1], scale=l0g[:, k:k + 1]),
                                     [pb_, Bc], [B(f"xT{b}")])

            xTB = [B(f"xT{b}") for b in range(4)]
            bigB = [B(f"bigA{b}") for b in range(4)]

            def x_load(ti, b):
                s_ = b % 2
                bx = B(f"xin{s_}")
                S.dma(xin[s_][:], xw[(ti * 4 + b) * 128:(ti * 4 + b + 1) * 128, :], [], [bx], bx)

            def x_cast(b):
                s_ = b % 2
                if b % 2 == 0:
                    S.op("act", lambda e, s_=s_: e.activation(out=xb[s_][:], in_=xin[s_][:], func=AF.Copy), [B(f"xin{s_}")], [B(f"xb{s_}")])
                else:
                    S.op("dve", lambda e, s_=s_: e.tensor_copy(out=xb[s_][:], in_=xin[s_][:]), [B(f"xin{s_}")], [B(f"xb{s_}")])

            for i in range(NT if stage >= 2 else 0):
                if i == 0:
                    x_load(0, 0)
                    x_load(0, 1)
                    x_cast(0)
                    x_cast(1)
                    x_load(0, 2)
                    x_load(0, 3)
                transpose_block(xb[0], B("xb0"), 0, False)
                transpose_block(xb[1], B("xb1"), 1, False)
                x_cast(2)
                x_cast(3)
                transpose_block(xb[0], B("xb0"), 2, False)
                transpose_block(xb[1], B("xb1"), 3, False)
                if i + 1 < NT:
                    x_load(i + 1, 0)
                    x_load(i + 1, 1)
                for n in range(8):
                    W, WB = w_next()
                    for b in range(4):
                        pt, pb_ = nps()
                        for k in range(16):
                            S.op("pe", lambda e, k=k, b=b, W=W, pt=pt: e.matmul(pt[:], lhsT=xT[:, k, b * 128:(b + 1) * 128], rhs=W[:, k, :],
                                                                          start=(k == 0), stop=(k == 15)), [WB, xTB[b]], [pb_], inc=(k == 15))
                        S.op("act", lambda e, b=b, n=n, pt=pt: e.activation(out=bigA[:, b, n * 512:(n + 1) * 512], in_=pt[:], func=AF.Gelu_apprx_tanh),
                             [pb_], [bigB[b]])
                        S.op("dve", lambda e, b=b, n=n: e.bn_stats(out=stats[:, b, n, :], in_=bigA[:, b, n * 512:(n + 1) * 512]), [bigB[b]], [B(f"stats{b}")])
                    w_done()
                for b in range(4):
                    S.op("dve", lambda e, b=b: e.bn_aggr(out=mv[:, b, :], in_=stats[:, b, :, :].rearrange("p a c -> p (a c)")), [B(f"stats{b}")], [B(f"mv{b}")])
                    S.op("act", lambda e, b=b: e.activation(out=rstd[:, b:b + 1], in_=mv[:, b, 1:2], func=AF.Ln, bias=epsc[:, 0:1]), [B(f"mv{b}"), Bc], [B(f"rstd{b}")])
                    S.op("act", lambda e, b=b: e.activation(out=rstd[:, b:b + 1], in_=rstd[:, b:b + 1], func=AF.Exp, scale=-0.5), [], [B(f"rstd{b}")])
                    S.op("dve", lambda e, b=b: e.tensor_scalar(out=bigA[:, b, :], in0=bigA[:, b, :], scalar1=mv[:, b, 0:1], scalar2=rstd[:, b:b + 1],
                                                               op0=ALU.subtract, op1=ALU.mult), [B(f"mv{b}"), B(f"rstd{b}")], [bigB[b]])
                for g in range(8):
                    for which in range(2):
                        W, WB = w_next()
                        for jj in range(4):
                            j = g * 4 + jj
                            pt, pb_ = nps()
                            for k in range(16):
                                S.op("pe", lambda e, k=k, jj=jj, W=W, pt=pt: e.matmul(pt[:], lhsT=W[:, k, jj * 128:(jj + 1) * 128], rhs=xT[:, k, :],
                                                                                  start=(k == 0), stop=(k == 15)), [WB] + xTB, [pb_], inc=(k == 15))
                            if which == 0:
                                S.op("act", lambda e, j=j, pt=pt: e.activation(out=pbuf[:, j, :], in_=pt[:], func=AF.Gelu_apprx_tanh), [pb_], [B(f"p{j}")])
                            else:
                                t = alt(tmpb)
                                S.op("act", lambda e, t=t, pt=pt: e.activation(out=tmpb[t][:], in_=pt[:], func=AF.Silu), [pb_], [B(f"tmpb{t}")])
                                S.op("pool", lambda e, t=t, j=j: e.tensor_tensor(out=pbuf[:, j, :], in0=pbuf[:, j, :], in1=tmpb[t][:], op=ALU.mult),
                                     [B(f"tmpb{t}")], [B(f"p{j}")])
                        w_done()
                for j in range(32):
                    g = j // 2
                    pt, pb_ = nps()
                    for b in range(4):
                        S.op("pe", lambda e, b=b, j=j, g=g, pt=pt: e.matmul(pt[:, b * 128:(b + 1) * 128], lhsT=bigA[:, b, j * 128:(j + 1) * 128], rhs=WsTb[:, g, :],
                                                                      start=True, stop=True), [bigB[b], B("WsTb")], [pb_], inc=(b == 3))
                    t = alt(tmpf)
                    S.op("dve", lambda e, t=t, j=j, pt=pt: e.scalar_tensor_tensor(out=tmpf[t][:].rearrange("p (a c) -> p a c", a=4),
                                                                             in0=pt[:].rearrange("p (a c) -> p a c", a=4), scalar=lnvg[:, j:j + 1],
                                                                             in1=Bias[:, j:j + 1, :].broadcast_to([128, 4, 128]), op0=ALU.mult, op1=ALU.add),
                         [pb_, B("Bias"), Bc], [B(f"tmpf{t}")])
                    S.op("pool", lambda e, t=t, j=j: e.tensor_tensor(out=pbuf[:, j, :], in0=pbuf[:, j, :], in1=tmpf[t][:], op=ALU.mult),
                         [B(f"tmpf{t}")], [B(f"p{j}")])
                rA = [bigA[:, b, :].bitcast(F32) for b in range(4)]
                for b in range(4):
                    S.dma(rA[b], xw[(i * 4 + b) * 128:(i * 4 + b + 1) * 128, :], [], [bigB[b]], bigB[b])
                pB = [B(f"p{j}") for j in range(32)]
                for n in range(4):
                    pts = [nps() for _ in range(4)]
                    for hf in range(2):
                        W, WB = w_next()
                        for b in range(4):
                            pt, pb_ = pts[b]
                            for jj in range(16):
                                j = hf * 16 + jj
                                S.op("pe", lambda e, b=b, j=j, jj=jj, W=W, pt=pt, hf=hf: e.matmul(pt[:], lhsT=pbuf[:, j, b * 128:(b + 1) * 128], rhs=W[:, jj, :],
                                                                                          start=(hf == 0 and jj == 0), stop=(hf == 1 and jj == 15)),
                                     [WB] + pB, [pb_], inc=(jj == 15))
                        w_done()
                    for b in range(4):
                        pt, pb_ = pts[b]
                        S.op("dve", lambda e, b=b, n=n, pt=pt: e.scalar_tensor_tensor(out=rA[b][:, n * 512:(n + 1) * 512], in0=rA[b][:, n * 512:(n + 1) * 512],
                                                                                 scalar=ALPHA, in1=pt[:], op0=ALU.mult, op1=ALU.add), [pb_], [bigB[b]])
                        S.op("dve", lambda e, b=b, n=n: e.bn_stats(out=stats[:, b, n, :], in_=rA[b][:, n * 512:(n + 1) * 512]), [bigB[b]], [B(f"stats{b}")])
                for b in range(4):
                    S.op("dve", lambda e, b=b: e.bn_aggr(out=mv[:, b, :], in_=stats[:, b, 0:4, :].rearrange("p a c -> p (a c)")), [B(f"stats{b}")], [B(f"mv{b}")])
                    S.op("act", lambda e, b=b: e.activation(out=rstd[:, b:b + 1], in_=mv[:, b, 1:2], func=AF.Ln, bias=epsc[:, 0:1]), [B(f"mv{b}"), Bc], [B(f"rstd{b}")])
                    S.op("act", lambda e, b=b: e.activation(out=rstd[:, b:b + 1], in_=rstd[:, b:b + 1], func=AF.Exp, scale=-0.5), [], [B(f"rstd{b}")])
                    S.op("dve", lambda e, b=b: e.tensor_scalar(out=rA[b], in0=rA[b], scalar1=mv[:, b, 0:1], scalar2=rstd[:, b:b + 1],
                                                               op0=ALU.subtract, op1=ALU.mult), [B(f"mv{b}"), B(f"rstd{b}")], [bigB[b]])
                    S.dma(x1h_s[(i * 4 + b) * 128:(i * 4 + b + 1) * 128, :], rA[b], [bigB[b]], [B(f"x1h{i * 4 + b}")], bigB[b])
                    s = b % 2
                    S.op("act", lambda e, b=b, s=s: e.activation(out=xb[s][:], in_=rA[b], func=AF.Copy), [bigB[b]], [B(f"xb{s}")])
                    transpose_block(xb[s], B(f"xb{s}"), b, True)
                own_tile = T0 <= i < T1
                for gi, g in enumerate(list(range(0, 4)) + list(range(12, 20))):
                    if not own_tile and g not in (2, 3):
                        continue
                    W, WB = w_next()
                    sg = alt(stg)
                    bsg = B(f"stg{sg}")
                    for jj in range(4):
                        pt, pb_ = nps()
                        for k in range(16):
                            S.op("pe", lambda e, k=k, jj=jj, W=W, pt=pt: e.matmul(pt[:], lhsT=W[:, k, jj * 128:(jj + 1) * 128], rhs=xT[:, k, :],
                                                                              start=(k == 0), stop=(k == 15)), [WB] + xTB, [pb_], inc=(k == 15))
                        if g < 2:
                            S.op("act", lambda e, jj=jj, sg=sg, pt=pt: e.activation(out=stg[sg][:, jj, :], in_=pt[:], func=AF.Copy, scale=0.0625), [pb_], [bsg])
                        elif g < 4:
                            S.op("act", lambda e, jj=jj, sg=sg, pt=pt: e.activation(out=stg[sg][:, jj, :], in_=pt[:], func=AF.Copy), [pb_], [bsg])
                        else:
                            S.op("act", lambda e, jj=jj, sg=sg, pt=pt: e.activation(out=stg[sg][:, jj, :], in_=pt[:], func=AF.Silu), [pb_], [bsg])
                            jz = (g - 12) * 4 + jj
                            S.op("dve", lambda e, jj=jj, sg=sg, jz=jz: e.tensor_scalar(out=stg[sg][:, jj, :], in0=stg[sg][:, jj, :], scalar1=gncol[:, jz:jz + 1],
                                                                                     scalar2=None, op0=ALU.mult), [Bc], [bsg])
                    w_done()
                    for b in range(4):
                        ch = i * 4 + b
                        if g < 2:
                            dst = qT_s[ch, :, g * 4:(g + 1) * 4, :]
                        elif g < 4:
                            dst = kT_s[ch, :, (g - 2) * 4:(g - 1) * 4, :]
                        else:
                            dst = zT_s[ch, :, (g - 12) * 4:(g - 11) * 4, :]
                        S.dma(dst, stg[sg][:, :, b * 128:(b + 1) * 128], [bsg], [B(f"fm{ch}")], bsg)
                for g in range(2, 12):
                    W, WB = w_next()
                    sg = alt(stg)
                    bsg = B(f"stg{sg}")
                    for b in range(4):
                        pt, pb_ = nps()
                        for k in range(16):
                            S.op("pe", lambda e, k=k, b=b, W=W, pt=pt: e.matmul(pt[:], lhsT=xT[:, k, b * 128:(b + 1) * 128], rhs=W[:, k, :],
                                                                          start=(k == 0), stop=(k == 15)), [WB, xTB[b]], [pb_], inc=(k == 15))
                        S.op("act", lambda e, b=b, sg=sg, pt=pt: e.activation(out=stg[sg][:, b, :], in_=pt[:], func=AF.Copy), [pb_], [bsg])
                    w_done()
                    if g < 4:
                        dst = ktok_s[i * 512:(i + 1) * 512, (g - 2) * 512:(g - 1) * 512]
                    else:
                        dst = v_s[i * 512:(i + 1) * 512, (g - 4) * 512:(g - 3) * 512]
                    S.dma(dst.rearrange("(b p) c -> p b c", p=128), stg[sg][:], [bsg], [B(f"tm{i}")], bsg)
                pt, pb_ = nps()
                for k in range(16):
                    S.op("pe", lambda e, k=k, pt=pt: e.matmul(pt[0:32, :], lhsT=wgl[:, k, :], rhs=xT[:, k, :], start=(k == 0), stop=(k == 15)),
                         [Bc] + xTB, [pb_], inc=(k == 15))
                S.op("dve", lambda e, pt=pt: e.tensor_copy(out=glh[0:32, :], in_=pt[0:32, :]), [pb_], [B("glT")])
                S.op("dve", lambda e, pt=pt: e.tensor_tensor(out=gll[0:32, :], in0=pt[0:32, :], in1=glh[0:32, :], op=ALU.subtract), [pb_], [B("glT")])
                for b in range(4):
                    for c4 in range(4):
                        pt, pb_ = nps()
                        bs_ = slice(b * 128, (b + 1) * 128)
                        cs_ = slice(c4 * 512, (c4 + 1) * 512)
                        S.op("pe", lambda e, bs_=bs_, cs_=cs_, pt=pt: e.matmul(pt[:], lhsT=glh[0:33, bs_], rhs=W2h[0:33, cs_], start=True, stop=False),
                             [B("glT"), B("W2h"), B("W2l")], [pb_], inc=False)
                        S.op("pe", lambda e, bs_=bs_, cs_=cs_, pt=pt: e.matmul(pt[:], lhsT=gll[0:33, bs_], rhs=W2h[0:33, cs_], start=False, stop=False),
                             [B("glT"), B("W2h"), B("W2l")], [pb_], inc=False)
                        S.op("pe", lambda e, bs_=bs_, cs_=cs_, pt=pt: e.matmul(pt[:], lhsT=glh[0:33, bs_], rhs=W2l[0:33, cs_], start=False, stop=True),
                             [B("glT"), B("W2h"), B("W2l")], [pb_])
                        t = alt(tmpf)
                        sf = alt(stf)
                        sq = alt(sps)
                        S.op("act", lambda e, t=t, pt=pt: e.activation(out=tmpf[t][:], in_=pt[:], func=AF.Exp, scale=-1.0), [pb_], [B(f"tmpf{t}")])
                        S.op("act", lambda e, t=t, sf=sf: e.activation(out=stf[sf][:], in_=tmpf[t][:], func=AF.Ln, bias=1.0), [B(f"tmpf{t}")], [B(f"stf{sf}")])
                        S.op("pool", lambda e, sf=sf, sq=sq: e.tensor_copy(out=sps[sq][:, 0, :], in_=stf[sf][:]), [B(f"stf{sf}")], [B(f"sps{sq}")])
                        S.op("dve", lambda e, sf=sf, sq=sq: e.tensor_tensor(out=sps[sq][:, 1, :], in0=stf[sf][:], in1=sps[sq][:, 0, :], op=ALU.subtract),
                             [B(f"stf{sf}")], [B(f"sps{sq}")])
                        d_ = c4 // 2
                        r_ = slice((i * 4 + b) * 128, (i * 4 + b + 1) * 128)
                        dst = sp_s[r_, d_ * 2048:(d_ + 1) * 2048].rearrange("p (a c) -> p a c", a=2)[:, :, (c4 % 2) * 512:(c4 % 2 + 1) * 512]
                        S.dma(dst, sps[sq][:], [B(f"sps{sq}")], [B(f"sp{i * 4 + b}")], B(f"sps{sq}"))
                if i + 1 < NT:
                    x_cast(0)
                    x_cast(1)
                    x_load(i + 1, 2)
                    x_load(i + 1, 3)
            S.barrier()

        prep_banks = [0, 1, 2, 3]
        head_banks = [4, 5, 6, 7]
        pst2 = {"p": 0, "h": 0}

        def nps_p():
            i = prep_banks[pst2["p"] % 4]
            pst2["p"] += 1
            return psum[i], psb[i]

        def nps_h():
            i = head_banks[pst2["h"] % 4]
            pst2["h"] += 1
            return psum[i], psb[i]

        with ExitStack() as es2:
            E2 = es2.enter_context
            Sst = E2(nc.sbuf_tensor("Sst", [128, 8, 1024], F32))
            Sbf = E2(nc.sbuf_tensor("Sbf", [128, 8, 1024], BF16))
            sp_t = E2(nc.sbuf_tensor("sp_t", [128, 2048], BF16))
            qT_t = E2(nc.sbuf_tensor("qT_t", [128, 8, 128], BF16))
            kT_t = E2(nc.sbuf_tensor("kT_t", [128, 8, 128], BF16))
            ktok_t = E2(nc.sbuf_tensor("ktok_t", [128, 1024], BF16))
            E4 = E2(nc.sbuf_tensor("E4", [128, 1024], F32))
            Ea = E2(nc.sbuf_tensor("Ea", [128, 3, 8, 128], F32))
            Kt = E2(nc.sbuf_tensor("Kt", [128, 1024], BF16))
            v_t = [E2(nc.sbuf_tensor(f"v_t{i}", [128, DI], BF16)) for i in range(2)]
            Kd = [E2(nc.sbuf_tensor(f"Kd{i}", [128, 1024], BF16)) for i in range(2)]
            Pa = [E2(nc.sbuf_tensor(f"Pa{i}", [128, 3, 8, 128], BF16)) for i in range(2)]
            scp = [E2(nc.sbuf_tensor(f"scp{i}", [128, 24], F32)) for i in range(2)]
            zT_t = E2(nc.sbuf_tensor("zT_t", [128, 32, 128], BF16))
            of_t = [E2(nc.sbuf_tensor(f"of_t{i}", [128, 1024], F32)) for i in range(2)]
            osb = [E2(nc.sbuf_tensor(f"osb{i}", [128, 1024], F32)) for i in range(2)]
            junk = E2(nc.sbuf_tensor("junk", [128, 1024], BF16))
            aTs = [E2(nc.sbuf_tensor(f"aTs{i}", [128, 4, 128], BF16)) for i in range(2)]
            sc = E2(nc.sbuf_tensor("sc", [128, 8], F32))
            onb = E2(nc.sbuf_tensor("onb", [128, DI], BF16))
            yT = [E2(nc.sbuf_tensor(f"yT{i}", [128, 32, 128], BF16)) for i in range(2)]

            def prep(n, d, do_out, p):
                nn = n + 1 if d == 0 else n - 1
                has_next = 0 <= nn < NCH
                last_col = 127 if d == 0 else 0
                Bsp, Bq, Bk, Bkt, Bv = B("sp_t"), B("qT_t"), B("kT_t"), B("ktok_t"), B(f"v_t{p}")
                r0 = n * 128
                S.dma(sp_t[:], sp_s[r0:r0 + 128, d * 2048:(d + 1) * 2048], [B(f"sp{n}")], [Bsp], Bsp)
                S.dma(ktok_t[:], ktok_s[r0:r0 + 128, :], [B(f"tm{n // 4}")], [Bkt], Bkt)
                S.dma(v_t[p][:], v_s[r0:r0 + 128, :], [B(f"tm{n // 4}")], [Bv], Bv)
                if do_out:
                    S.dma(qT_t[:], qT_s[n], [B(f"fm{n}")], [Bq], Bq)
                    S.dma(kT_t[:], kT_s[n], [B(f"fm{n}")], [Bk], Bk)
                gts = []
                for half in range(2):
                    pt, pb_ = nps_p()
                    for c4 in range(4):
                        dc = half * 4 + c4
                        S.op("pe", lambda e, dc=dc, c4=c4, pt=pt: e.matmul(pt[:, c4 * 128:(c4 + 1) * 128], lhsT=sp_t[:, dc * 128:(dc + 1) * 128], rhs=tri[d],
                                                                      start=True, stop=False), [Bsp, Bc], [pb_], inc=False)
                        S.op("pe", lambda e, dc=dc, c4=c4, pt=pt: e.matmul(pt[:, c4 * 128:(c4 + 1) * 128], lhsT=sp_t[:, 1024 + dc * 128:1024 + (dc + 1) * 128], rhs=tri[d],
                                                                      start=False, stop=True), [Bsp, Bc], [pb_], inc=(c4 == 3))
                    gts.append((pt, pb_))
                grs = []
                for half in range(2):
                    pt, pb_ = nps_p()
                    S.op("pe", lambda e, half=half, pt=pt: e.matmul(pt[:], lhsT=triR[d], rhs=sp_t[:, half * 512:(half + 1) * 512], start=True, stop=False),
                         [Bsp, Bc], [pb_], inc=False)
                    S.op("pe", lambda e, half=half, pt=pt: e.matmul(pt[:], lhsT=triR[d], rhs=sp_t[:, 1024 + half * 512:1024 + (half + 1) * 512], start=False, stop=True),
                         [Bsp, Bc], [pb_])
                    grs.append((pt, pb_))
                Bsc = B(f"scp{p}")
                scq = scp[p]
                for half in range(2):
                    pt, pb_ = gts[half]
                    pv = pt[:].rearrange("p (a c) -> p a c", a=4)
                    S.op("act", lambda e, half=half, pv=pv: e.activation(out=scq[:, 16 + half * 4:20 + half * 4], in_=pv[:, :, last_col], func=AF.Exp), [pb_], [Bsc])
                BE4, BKd, BKt = B("E4"), B(f"Kd{p}"), B("Kt")
                if has_next:
                    kcol = keep[d][:, nn:nn + 1]
                    S.op("dve", lambda e, kcol=kcol: e.tensor_scalar(out=scq[:, 16:24], in0=scq[:, 16:24], scalar1=kcol, scalar2=None, op0=ALU.mult), [Bc], [Bsc])
                    for half in range(2):
                        pt, pb_ = grs[half]
                        S.op("act", lambda e, half=half, pt=pt: e.activation(out=E4[:, half * 512:(half + 1) * 512], in_=pt[:], func=AF.Exp), [pb_], [BE4])
                    S.op("dve", lambda e, kcol=kcol: e.scalar_tensor_tensor(out=Kd[p][:], in0=ktok_t[:], scalar=kcol, in1=E4[:], op0=ALU.mult, op1=ALU.mult),
                         [Bkt, BE4, Bc], [BKd])
                BEa = B("Ea")
                if do_out:
                    for half in range(2):
                        pt, pb_ = gts[half]
                        S.op("act", lambda e, half=half, pt=pt: e.activation(out=Ea[:, 2, half * 4:(half + 1) * 4, :].rearrange("p a c -> p (a c)"), in_=pt[:], func=AF.Exp),
                             [pb_], [BEa])
                    gcs = []
                    for half in range(2):
                        pt, pb_ = nps_p()
                        for c4 in range(4):
                            dc = half * 4 + c4
                            S.op("pe", lambda e, dc=dc, c4=c4, pt=pt: e.matmul(pt[:, c4 * 128:(c4 + 1) * 128], lhsT=sp_t[:, dc * 128:(dc + 1) * 128], rhs=triC[d],
                                                                          start=True, stop=False), [Bsp, Bc], [pb_], inc=False)
                            S.op("pe", lambda e, dc=dc, c4=c4, pt=pt: e.matmul(pt[:, c4 * 128:(c4 + 1) * 128], lhsT=sp_t[:, 1024 + dc * 128:1024 + (dc + 1) * 128], rhs=triC[d],
                                                                          start=False, stop=True), [Bsp, Bc], [pb_], inc=(c4 == 3))
                        gcs.append((pt, pb_))
                    for half in range(2):
                        pc, pcb = gcs[half]
                        S.op("act", lambda e, half=half, pc=pc: e.activation(out=Ea[:, 0, half * 4:(half + 1) * 4, :].rearrange("p a c -> p (a c)"), in_=pc[:], func=AF.Exp),
                             [pcb], [BEa])
                        S.op("act", lambda e, half=half, pc=pc: e.activation(out=Ea[:, 1, half * 4:(half + 1) * 4, :].rearrange("p a c -> p (a c)"), in_=pc[:], func=AF.Exp,
                                                                          scale=-1.0), [pcb], [BEa])
                    BP0, BP1, BP2 = B(f"Pa{p}_0"), B(f"Pa{p}_1"), B(f"Pa{p}_2")
                    S.op("dve", lambda e: e.tensor_tensor(out=Pa[p][:, 0, :, :], in0=qT_t[:], in1=Ea[:, 0, :, :], op=ALU.mult), [Bq, BEa], [BP0])
                    S.op("dve", lambda e: e.tensor_tensor(out=Pa[p][:, 1, :, :], in0=kT_t[:], in1=Ea[:, 1, :, :], op=ALU.mult), [Bk, BEa], [BP1])
                    S.op("dve", lambda e: e.tensor_tensor(out=Pa[p][:, 2, :, :], in0=qT_t[:], in1=Ea[:, 2, :, :], op=ALU.mult), [Bq, BEa], [BP2])
                    for h in range(4):
                        pa, pab = nps_p()
                        for c2 in range(2):
                            dc = 2 * h + c2
                            S.op("pe", lambda e, dc=dc, c2=c2, pa=pa: e.matmul(pa[:, 0:128], lhsT=Pa[p][:, 1, dc, :], rhs=Pa[p][:, 0, dc, :], start=(c2 == 0), stop=(c2 == 1)),
                                 [BP0, BP1], [pab], inc=(c2 == 1))
                        S.op("dve", lambda e, h=h, pa=pa: e.tensor_tensor(out=aTs[p][:, h, :], in0=pa[:, 0:128], in1=maskT[d], op=ALU.mult), [pab, Bc], [B(f"aTs{p}_{h}")])

            def heads(n, d, do_out, final, p):
                nn = n + 1 if d == 0 else n - 1
                has_next = 0 <= nn < NCH
                r0 = n * 128
                Bv, BKd, Bsc = B(f"v_t{p}"), B(f"Kd{p}"), B(f"scp{p}")
                vt, Pq, Kq, scq = v_t[p], Pa[p], Kd[p], scp[p]
                for h in range(4):
                    if do_out:
                        Ba = B(f"aTs{p}_{h}")
                        aT_ap = aTs[p][:, h, :]
                        oi = alt(osb)
                        Bo = B(f"osb{oi}")
                        if final:
                            Bof = B(f"of_t{oi}")
                            S.dma(of_t[oi][:], of_s[r0:r0 + 128, h * 1024:(h + 1) * 1024], [B(f"of{n}")], [Bof], Bof)
                        for half in range(2):
                            po, pob = nps_h()
                            c0 = h * 1024 + half * 512
                            S.op("pe", lambda e, aT_ap=aT_ap, c0=c0, po=po: e.matmul(po[:], lhsT=aT_ap, rhs=vt[:, c0:c0 + 512], start=True, stop=False),
                                 [Ba, Bv], [pob], inc=False)
                            for c2 in range(2):
                                dc = 2 * h + c2
                                S.op("pe", lambda e, dc=dc, c2=c2, half=half, po=po: e.matmul(po[:], lhsT=Pq[:, 2, dc, :], rhs=Sbf[:, dc, half * 512:(half + 1) * 512],
                                                                                       start=False, stop=(c2 == 1)), [B(f"Pa{p}_2"), B(f"Sbf{dc}")], [pob], inc=(c2 == 1))
                            if not final:
                                S.op("act", lambda e, oi=oi, half=half, po=po: e.activation(out=osb[oi][:, half * 512:(half + 1) * 512], in_=po[:], func=AF.Copy), [pob], [Bo])
                            else:
                                S.op("dve", lambda e, oi=oi, half=half, po=po: e.tensor_tensor(out=osb[oi][:, half * 512:(half + 1) * 512], in0=po[:],
                                                                                           in1=of_t[oi][:, half * 512:(half + 1) * 512], op=ALU.add), [pob, Bof], [Bo])
                        if not final:
                            S.dma(of_s[r0:r0 + 128, h * 1024:(h + 1) * 1024], osb[oi][:], [Bo], [B(f"of{n}")], Bo)
                        else:
                            S.op("act", lambda e, oi=oi, h=h: e.activation(out=junk[:], in_=osb[oi][:], func=AF.Square, accum_out=sc[:, h:h + 1]), [Bo], [B("junk"), B(f"ss{h}")])
                            S.op("act", lambda e, h=h: e.activation(out=sc[:, 4 + h:5 + h], in_=sc[:, h:h + 1], func=AF.Ln, bias=epsc[:, 1:2], scale=1.0 / 1024.0),
                                 [B(f"ss{h}"), Bc], [B(f"rs{h}")])
                            S.op("act", lambda e, h=h: e.activation(out=sc[:, 4 + h:5 + h], in_=sc[:, 4 + h:5 + h], func=AF.Exp, scale=-0.5),
                                 [], [B(f"rs{h}")])
                            S.op("act", lambda e, oi=oi, h=h: e.activation(out=onb[:, h * 1024:(h + 1) * 1024], in_=osb[oi][:], func=AF.Copy, scale=sc[:, 4 + h:5 + h]),
                                 [Bo, B(f"rs{h}")], [B("onb")])
                    if has_next:
                        for c2 in range(2):
                            dc = 2 * h + c2
                            for half in range(2):
                                pu, pub = nps_h()
                                c0 = h * 1024 + half * 512
                                S.op("pe", lambda e, dc=dc, c0=c0, pu=pu: e.matmul(pu[:], lhsT=Kq[:, dc * 128:(dc + 1) * 128], rhs=vt[:, c0:c0 + 512], start=True, stop=True),
                                     [BKd, Bv], [pub])
                                S.op("dve", lambda e, dc=dc, half=half, pu=pu: e.scalar_tensor_tensor(out=Sst[:, dc, half * 512:(half + 1) * 512],
                                                                                                 in0=Sst[:, dc, half * 512:(half + 1) * 512], scalar=scq[:, 16 + dc:17 + dc],
                                                                                                 in1=pu[:], op0=ALU.mult, op1=ALU.add), [pub, Bsc], [B(f"Sst{dc}")])
                            S.op("act", lambda e, dc=dc: e.activation(out=Sbf[:, dc, :], in_=Sst[:, dc, :], func=AF.Copy), [B(f"Sst{dc}")], [B(f"Sbf{dc}")])
                if do_out and final:
                    Bz = B("zT_t")
                    S.dma(zT_t[:], zT_s[n], [B(f"fm{n}")], [Bz], Bz)
                    yi = alt(yT)
                    ByT = B(f"yT{yi}")
                    for j4 in range(4):
                        pt, pb_ = nps_h()
                        ptb = pt[:].bitcast(BF16)
                        for jj in range(8):
                            j = j4 * 8 + jj
                            S.op("pe", lambda e, j=j, jj=jj, ptb=ptb: e.transpose(out=ptb[:, jj * 128:(jj + 1) * 128], in_=onb[:, j * 128:(j + 1) * 128], identity=identb[:]),
                                 [B("onb"), Bc], [pb_], inc=(jj == 7))
                        S.op("dve", lambda e, j4=j4, ptb=ptb, yi=yi: e.tensor_tensor(out=yT[yi][:, j4 * 8:(j4 + 1) * 8, :].rearrange("p a c -> p (a c)"), in0=ptb[:, :],
                                                                                 in1=zT_t[:, j4 * 8:(j4 + 1) * 8, :].rearrange("p a c -> p (a c)"), op=ALU.mult), [pb_, Bz], [ByT])
                    S.dma(yT_s[n], yT[yi][:], [ByT], [B(f"yTs{n}")], ByT)

            def run_scan(chunks, d, out_pred, final):
                S.op("dve", lambda e: e.memset(Sst[:], 0.0), [], [B(f"Sst{dc}") for dc in range(8)])
                S.op("pool", lambda e: e.memset(Sbf[:], 0.0), [], [B(f"Sbf{dc}") for dc in range(8)])
                if not chunks:
                    return
                prep(chunks[0], d, out_pred(chunks[0]), 0)
                for i, n in enumerate(chunks):
                    if i + 1 < len(chunks):
                        prep(chunks[i + 1], d, out_pred(chunks[i + 1]), (i + 1) % 2)
                    heads(n, d, out_pred(n), final, i % 2)

            run_scan(list(range(0, T1 * 4)) if stage >= 3 else [], 0, lambda n: n >= T0 * 4, False)
            S.barrier()
            run_scan(list(range(NCH - 1, T0 * 4 - 1, -1)) if stage >= 4 else [], 1, lambda n: n < T1 * 4, True)
            S.barrier()

        with ExitStack() as es4:
            E4_ = es4.enter_context
            yTt2 = [E4_(nc.sbuf_tensor(f"yTt{i}", [128, 4, 32, 128], BF16)) for i in range(2)]
            rr4 = E4_(nc.sbuf_tensor("rr4", [128, 4, D], F32))
            wo = [E4_(nc.sbuf_tensor(f"wo{i}", [128, 16, 512], BF16)) for i in range(3)]
            bcg = [E4_(nc.sbuf_tensor(f"bcg{i}", [128, D], F32)) for i in range(4)]
            st4 = E4_(nc.sbuf_tensor("st4", [128, 4, 4, 6], F32))
            mv4 = E4_(nc.sbuf_tensor("mv4", [128, 4, 2], F32))
            Bbc = B("bcg")
            S.dma(bcg[0][:], ln_g[0:1, :].broadcast_to([128, D]), [], [Bbc], Bbc)
            S.dma(bcg[1][:], ln_b[0:1, :].broadcast_to([128, D]), [], [Bbc], Bbc)
            S.dma(bcg[2][:], ln_g[1:2, :].broadcast_to([128, D]), [], [Bbc], Bbc)
            S.dma(bcg[3][:], ln_b[1:2, :].broadcast_to([128, D]), [], [Bbc], Bbc)
            wo_k = {"issued": 0, "used": 0}
            tiles4 = list(range(T0, T1)) if stage >= 5 else []
            tot4 = len(tiles4) * 8

            def wo_issue():
                k = wo_k["issued"]
                if k >= tot4:
                    return
                s_ = k % 3
                Bw = B(f"wo{s_}")
                S.dma(wo[s_][:].rearrange("p k c -> p (k c)"), wB_out_s[k % 8], [B("wscr")], [Bw], Bw)
                wo_k["issued"] = k + 1

            def y_load(ti4):
                if ti4 >= len(tiles4):
                    return
                i_ = tiles4[ti4]
                for b in range(4):
                    n = i_ * 4 + b
                    By = B(f"yTt{ti4 % 2}_{b}")
                    S.dma(yTt2[ti4 % 2][:, b, :, :], yT_s[n], [B(f"yTs{n}")], [By], By)

            y_load(0)
            for _ in range(2):
                wo_issue()
            for ti4, i in enumerate(tiles4):
                yTt = yTt2[ti4 % 2]
                y_load(ti4 + 1)
                for b in range(4):
                    n = i * 4 + b
                    Br = B(f"rr4{b}")
                    S.dma(rr4[:, b, :], x1h_s[n * 128:(n + 1) * 128, :], [B(f"x1h{n}")], [Br], Br)
                    S.op("pool", lambda e, b=b: e.tensor_tensor(out=rr4[:, b, :], in0=rr4[:, b, :], in1=bcg[0][:], op=ALU.mult), [Bbc], [Br])
                    S.op("pool", lambda e, b=b: e.tensor_tensor(out=rr4[:, b, :], in0=rr4[:, b, :], in1=bcg[1][:], op=ALU.add), [Bbc], [Br])
                for nt in range(4):
                    pts = [nps() for _ in range(4)]
                    for hf in range(2):
                        k = wo_k["used"]
                        while wo_k["issued"] <= k:
                            wo_issue()
                        wo_k["used"] = k + 1
                        s_ = k % 3
                        Bw = B(f"wo{s_}")
                        for b in range(4):
                            pt, pb_ = pts[b]
                            for jj in range(16):
                                j = hf * 16 + jj
                                S.op("pe", lambda e, b=b, j=j, jj=jj, s_=s_, pt=pt, hf=hf, yTt=yTt: e.matmul(pt[:], lhsT=yTt[:, b, j, :], rhs=wo[s_][:, jj, :],
                                                                                            start=(hf == 0 and jj == 0), stop=(hf == 1 and jj == 15)),
                                     [B(f"yTt{ti4 % 2}_{b}"), Bw], [pb_], inc=(jj == 15))
                        wo_issue()
                    for b in range(4):
                        pt, pb_ = pts[b]
                        Br = B(f"rr4{b}")
                        S.op("dve", lambda e, b=b, nt=nt, pt=pt: e.scalar_tensor_tensor(out=rr4[:, b, nt * 512:(nt + 1) * 512], in0=rr4[:, b, nt * 512:(nt + 1) * 512],
                                                                                   scalar=ALPHA, in1=pt[:], op0=ALU.mult, op1=ALU.add), [pb_], [Br])
                        S.op("dve", lambda e, b=b, nt=nt: e.bn_stats(out=st4[:, b, nt, :], in_=rr4[:, b, nt * 512:(nt + 1) * 512]), [Br], [B(f"st4{b}")])
                for b in range(4):
                    n = i * 4 + b
                    Br = B(f"rr4{b}")
                    Bm = B(f"mv4{b}")
                    S.op("dve", lambda e, b=b: e.bn_aggr(out=mv4[:, b, :], in_=st4[:, b, :, :].rearrange("p a c -> p (a c)")), [B(f"st4{b}")], [Bm])
                    S.op("act", lambda e, b=b: e.activation(out=mv4[:, b, 1:2], in_=mv4[:, b, 1:2], func=AF.Ln, bias=epsc[:, 0:1]), [Bc], [Bm])
                    S.op("act", lambda e, b=b: e.activation(out=mv4[:, b, 1:2], in_=mv4[:, b, 1:2], func=AF.Exp, scale=-0.5), [], [Bm])
                    S.op("dve", lambda e, b=b: e.tensor_scalar(out=rr4[:, b, :], in0=rr4[:, b, :], scalar1=mv4[:, b, 0:1], scalar2=mv4[:, b, 1:2], op0=ALU.subtract, op1=ALU.mult),
                         [Bm], [Br])
                    S.op("pool", lambda e, b=b: e.tensor_tensor(out=rr4[:, b, :], in0=rr4[:, b, :], in1=bcg[2][:], op=ALU.mult), [Bbc], [Br])
                    S.op("pool", lambda e, b=b: e.tensor_tensor(out=rr4[:, b, :], in0=rr4[:, b, :], in1=bcg[3][:], op=ALU.add), [Bbc], [Br])
                    o0 = n * 128 - T0 * 512
                    S.dma(yout[o0:o0 + 128, :], rr4[:, b, :], [Br], [B("yout")], Br)
            S.barrier()
        S.finish()
    return nc


def make_consts(NCH, keepf, keepb, colsT=None):
    c = np.zeros((128, 7 * 128 + 2 * NCH + 128 + 256), np.float32)
    s = np.arange(128)[:, None]
    t = np.arange(128)[None, :]
    c[:, 0:128] = np.eye(128, dtype=np.float32)
    c[:, 128:256] = np.where(s <= t, -1.0 / 16.0, 0.0)
    c[:, 256:384] = np.where(s > t, -1.0 / 16.0, 0.0)
    c[:, 384:512] = np.where(s >= t, -1.0 / 16.0, 0.0)
    c[:, 512:640] = np.where(s < t, -1.0 / 16.0, 0.0)
    c[:, 640:768] = np.where(s <= t, 1.0, 0.0)
    c[:, 768:896] = np.where(s > t, 1.0, 0.0)
    c[:, 896:896 + NCH] = keepf[None, :]
    c[:, 896 + NCH:896 + 2 * NCH] = keepb[None, :]
    if colsT is not None:
        c[:, 896 + 2 * NCH:896 + 2 * NCH + 128] = colsT
    o = 896 + 2 * NCH + 128
    c[:, o:o + 128] = c[:, 128:256] - c[:, 128 + 64:128 + 65]
    c[:, o + 128:o + 256] = c[:, 384:512] - c[:, 384 + 64:384 + 65]
    return c


def common_inputs(w_in_a, ln_v_g_a, ln_v_b_a, w_s_a, b_s_a, w_out_a, w_in_b, w_g2_b, b_g_b, gn_g_b, w_out_b, ln_g, ln_b):
    f = lambda a: np.ascontiguousarray(np.asarray(a, dtype=np.float32))
    return {
        "w_in_a": f(w_in_a[0]),
        "wsT": np.ascontiguousarray(f(w_s_a[0]).transpose(2, 0, 1).reshape(128, 2048)),
        "b_s_a": f(b_s_a[0]).reshape(1, 2048),
        "w_out_a": f(w_out_a[0]),
        "w_in_b": f(w_in_b[0]),
        "w_g2_b": f(w_g2_b[0]),
        "b_g_b": f(b_g_b[0]).reshape(1, 2048),
        "w_out_b": f(w_out_b[0]),
        "ln_g": f(ln_g),
        "ln_b": f(ln_b),
    }


def param_cols(ln_v_g_a, ln_v_b_a, gn_g_b, ln_g, ln_b):
    f = lambda a, n: np.asarray(a, np.float32).reshape(n, 128).T
    return np.ascontiguousarray(np.concatenate([f(ln_v_g_a[0], 32), f(ln_v_b_a[0], 32), f(gn_g_b[0], 32), f(ln_g[0], 16), f(ln_b[0], 16)], 1))


_NC_CACHE = {}


def kernel(x_prompt, x_sample, w_in_a, ln_v_g_a, ln_v_b_a, w_s_a, b_s_a, w_out_a,
           w_in_b, w_g2_b, b_g_b, gn_g_b, w_out_b, ln_g, ln_b):
    x_prompt = np.asarray(x_prompt, np.float32)
    x_sample = np.asarray(x_sample, np.float32)
    X = np.concatenate([x_prompt.reshape(-1, D), x_sample.reshape(-1, D)], 0)
    NTOK = X.shape[0]
    starts = {0, x_prompt.shape[0] * x_prompt.shape[1], x_prompt.shape[0] * x_prompt.shape[1] + x_sample.shape[1]}
    NT = (OWN + 2 * HALO) // 512
    NCH = NT * 4
    common = common_inputs(w_in_a, ln_v_g_a, ln_v_b_a, w_s_a, b_s_a, w_out_a, w_in_b, w_g2_b, b_g_b, gn_g_b, w_out_b, ln_g, ln_b)
    colsT = param_cols(ln_v_g_a, ln_v_b_a, gn_g_b, ln_g, ln_b)
    in_maps = []
    for c in range(NCORES):
        lo = c * OWN - HALO
        xw = np.zeros((NT * 512, D), np.float32)
        a, b = max(lo, 0), min(lo + NT * 512, NTOK)
        xw[a - lo:b - lo] = X[a:b]
        keepf = np.ones(NCH, np.float32)
        keepb = np.ones(NCH, np.float32)
        for n in range(NCH):
            g0 = lo + n * 128
            if g0 in starts or g0 <= 0 or g0 >= NTOK:
                keepf[n] = 0.0
            g1 = g0 + 128
            if g1 in starts or g1 <= 0 or g1 >= NTOK:
                keepb[n] = 0.0
        m = dict(common)
        m["xw"] = xw
        m["cst"] = make_consts(NCH, keepf, keepb, colsT)
        in_maps.append(m)
    key = (NT,)
    if key not in _NC_CACHE:
        _NC_CACHE[key] = build(NT, 1, NT - 1)
    nc = _NC_CACHE[key]
    res = run_bass_kernel_spmd(nc, in_maps, core_ids=list(range(NCORES)))
    Y = np.concatenate([np.asarray(res.results[c]["y"], np.float32) for c in range(NCORES)], 0)
    n_p = x_prompt.shape[0] * x_prompt.shape[1]
    y_prompt = Y[:n_p].reshape(x_prompt.shape)
    y_sample = Y[n_p:].reshape(x_sample.shape)
    return (y_prompt, y_sample)
```

```python
import os
import numpy as np
import ml_dtypes
KSKIP = os.environ.get('KSKIP', '')
from contextlib import ExitStack
import concourse.bass as bass
import concourse.mybir as mybir
from concourse.bass_utils import run_bass_kernel_spmd

F32 = mybir.dt.float32
BF16 = mybir.dt.bfloat16
AF = mybir.ActivationFunctionType
ALU = mybir.AluOpType

D = 2048
DI = 4096
ALPHA = 4.0 ** 0.25
LN_EPS = 1e-5
RMS_EPS = 1e-6
NCORES = 8
OWN = 5120
HALO = 512


class Buf:
    __slots__ = ("name", "w", "r", "dsem", "dcnt")

    def __init__(self, name):
        self.name = name
        self.w = None
        self.r = {}
        self.dsem = None
        self.dcnt = 0


class Sched:
    CE = ("pe", "act", "dve", "pool")

    def __init__(self, nc, es):
        self.nc = nc
        self.es = es
        self.q = {e: [] for e in ("pe", "act", "dve", "pool", "sp")}
        self.sems = {e: es.enter_context(nc.semaphore("sem_" + e)) for e in self.CE}
        self.cnt = {e: 0 for e in self.CE}
        self.seen = {e: {} for e in self.q}
        self.dbufs = []

    def _semof(self, k):
        return self.sems[k] if isinstance(k, str) else k.dsem

    def _need(self, e, toks):
        best = {}
        for t in toks:
            if t is None:
                continue
            k, v = t
            if k == e and e == "pe":
                continue
            if self.seen[e].get(k, 0) >= v:
                continue
            if best.get(k, 0) < v:
                best[k] = v
        for k, v in best.items():
            self.seen[e][k] = v
            sem = self._semof(k)
            self.q[e].append(lambda E, sem=sem, v=v: E.wait_ge(sem, v))

    def _deps(self, reads, writes):
        toks = []
        for b in reads:
            toks.append(b.w)
        for b in writes:
            toks.append(b.w)
            toks.extend(b.r.items())
        return toks

    def _update(self, tok, reads, writes):
        for b in writes:
            b.w = tok
            b.r = {}
        for b in reads:
            if b not in writes:
                k, v = tok
                if b.r.get(k, 0) < v:
                    b.r[k] = v

    mute = False

    def op(self, e, fn, reads=(), writes=(), inc=True):
        if self.mute:
            return
        self._need(e, self._deps(reads, writes))
        if inc:
            self.cnt[e] += 1
            v = self.cnt[e]
            sem = self.sems[e]
            self.q[e].append(lambda E, fn=fn, sem=sem: fn(E).then_inc(sem, 1))
            tok = (e, v)
        else:
            self.q[e].append(lambda E, fn=fn: fn(E))
            tok = (e, self.cnt[e] + 1)
        self._update(tok, reads, writes)

    def dma(self, out_ap, in_ap, reads, writes, sb, q="sp"):
        if self.mute:
            return
        if sb.dsem is None:
            sb.dsem = self.es.enter_context(self.nc.semaphore("d_" + sb.name))
            self.dbufs.append(sb)
        toks = self._deps(reads, writes)
        if sb.dcnt > 0:
            toks.append((sb, sb.dcnt))
        self._need(q, toks)
        sb.dcnt += 16
        v = sb.dcnt
        sem = sb.dsem
        self.q[q].append(lambda E, o=out_ap, i=in_ap, sem=sem: E.dma_start(out=o, in_=i).then_inc(sem, 16))
        self._update((sb, v), reads, writes)

    def barrier(self):
        toks = [(e, self.cnt[e]) for e in self.CE if self.cnt[e] > 0]
        toks += [(b, b.dcnt) for b in self.dbufs if b.dcnt > 0]
        for e in self.q:
            self._need(e, [t for t in toks if t[0] != e or e != "pe"])

    def finish(self):
        with self.nc.Block() as block:
            q = self.q

            @block.sync
            def _(E):
                for t in q["sp"]:
                    t(E)

            @block.tensor
            def _(E):
                for t in q["pe"]:
                    t(E)

            @block.scalar
            def _(E):
                for t in q["act"]:
                    t(E)

            @block.vector
            def _(E):
                for t in q["dve"]:
                    t(E)

            @block.gpsimd
            def _(E):
                for t in q["pool"]:
                    t(E)


def build(NT, T0, T1, stage=9):
    NW = NT * 512
    NCH = NT * 4
    NOWN = (T1 - T0) * 512
    nc = bass.Bass("TRN2", target_bir_lowering=False)

    def din(name, shape, dt=F32):
        return nc.dram_tensor(name, list(shape), dt, kind="ExternalInput").ap()

    def dscr(name, shape, dt):
        return nc.dram_tensor(name, list(shape), dt, kind="Internal").ap()

    xw = din("xw", [NW, D])
    w_in_a = din("w_in_a", [D, 3 * DI])
    wsT_in = din("wsT", [128, 2048])
    b_s = din("b_s_a", [1, 16 * 128])
    w_out_a = din("w_out_a", [DI, D])
    w_in_b = din("w_in_b", [D, 10272])
    w_g2 = din("w_g2_b", [2, 16, 1024])
    b_g = din("b_g_b", [1, 2048])
    w_out_b = din("w_out_b", [DI, D])
    ln_g = din("ln_g", [2, D])
    ln_b = din("ln_b", [2, D])
    CW = 7 * 128 + 2 * NCH + 128 + 256
    cst = din("cst", [128, CW])
    yout = nc.dram_tensor("y", [NOWN, D], F32, kind="ExternalOutput").ap()

    wA_in_s = dscr("wA_in_s", [24, 128, 16 * 512], BF16)
    wA_out_s = dscr("wA_out_s", [8, 128, 16 * 512], BF16)
    wB_in_s = dscr("wB_in_s", [20, 128, 16 * 512], BF16)
    wB_out_s = dscr("wB_out_s", [8, 128, 16 * 512], BF16)
    x1h_s = dscr("x1h_s", [NW, D], F32)
    qT_s = dscr("qT_s", [NCH, 128, 8, 128], BF16)
    kT_s = dscr("kT_s", [NCH, 128, 8, 128], BF16)
    zT_s = dscr("zT_s", [NCH, 128, 32, 128], BF16)
    ktok_s = dscr("ktok_s", [NW, 1024], BF16)
    v_s = dscr("v_s", [NW, DI], BF16)
    sp_s = dscr("sp_s", [NW, 4096], BF16)
    of_s = dscr("of_s", [NW, DI], F32)
    yT_s = dscr("yT_s", [NCH, 128, 32, 128], BF16)

    es_all = ExitStack()
    with es_all:
        S = Sched(nc, es_all)
        bufs = {}

        def B(name):
            if name not in bufs:
                bufs[name] = Buf(name.replace("/", "_").replace(":", "_"))
            return bufs[name]

        psum = [es_all.enter_context(nc.psum_tensor(f"ps{i}", [128, 512], F32)) for i in range(8)]
        psb = [Buf(f"ps{i}") for i in range(8)]
        pstate = {"i": 0}

        def nps():
            i = pstate["i"]
            pstate["i"] = (i + 1) % 8
            return psum[i], psb[i]

        E = es_all.enter_context
        cst_t = E(nc.sbuf_tensor("cst_t", [128, CW], F32))
        tribf = E(nc.sbuf_tensor("tribf", [128, 6, 128], BF16))
        identb = E(nc.sbuf_tensor("identb", [128, 128], BF16))
        cols = cst_t[:, 896 + 2 * NCH:896 + 2 * NCH + 128]
        Bc = B("consts")
        ident = cst_t[:, 0:128]
        tri = [tribf[:, 0, :], tribf[:, 2, :]]
        triR = [tribf[:, 1, :], tribf[:, 3, :]]
        maskT = [cst_t[:, 640:768], cst_t[:, 768:896]]
        keep = [cst_t[:, 896:896 + NCH], cst_t[:, 896 + NCH:896 + 2 * NCH]]

        epsc = E(nc.sbuf_tensor("epsc", [128, 2], F32))
        S.op("dve", lambda e: e.memset(epsc[:, 0:1], LN_EPS), [], [Bc])
        S.op("dve", lambda e: e.memset(epsc[:, 1:2], RMS_EPS), [], [Bc])
        S.dma(cst_t[:], cst[:, :], [], [Bc], Bc)
        S.op("dve", lambda e: e.tensor_copy(out=identb[:], in_=ident), [Bc], [Bc])
        S.op("dve", lambda e: e.tensor_copy(out=tribf[:, 0:4, :].rearrange("p a b -> p (a b)"), in_=cst_t[:, 128:640]), [Bc], [Bc])
        S.op("dve", lambda e: e.tensor_copy(out=tribf[:, 4:6, :].rearrange("p a b -> p (a b)"), in_=cst_t[:, CW - 256:CW]), [Bc], [Bc])
        triC = [tribf[:, 4, :], tribf[:, 5, :]]

        lnvg = cols[:, 0:32]
        lnvb = cols[:, 32:64]
        gncol = cols[:, 64:96]
        l0g = cols[:, 96:112]
        l0b = cols[:, 112:128]

        with ExitStack() as es0:
            S.mute = ('b' in KSKIP)
            E0 = es0.enter_context
            st32 = [E0(nc.sbuf_tensor(f"st32_{i}", [128, 8, 512], F32)) for i in range(6)]
            st16 = [E0(nc.sbuf_tensor(f"st16_{i}", [128, 8 * 512], BF16)) for i in range(6)]
            jobs = []
            wa = w_in_a.rearrange("(k p) n -> p k n", p=128)
            for g in range(24):
                for h in range(2):
                    jobs.append((wa[:, h * 8:(h + 1) * 8, g * 512:(g + 1) * 512], wA_in_s[g, :, h * 4096:(h + 1) * 4096]))
            wb = w_in_b.rearrange("(k p) n -> p k n", p=128)
            for g in range(20):
                for h in range(2):
                    jobs.append((wb[:, h * 8:(h + 1) * 8, g * 512:(g + 1) * 512], wB_in_s[g, :, h * 4096:(h + 1) * 4096]))
            for (wsrc, wdst) in ((w_out_a, wA_out_s), (w_out_b, wB_out_s)):
                wo = wsrc.rearrange("(j p) n -> p j n", p=128)
                for n in range(4):
                    for hf in range(2):
                        for qq in range(2):
                            j0 = hf * 16 + qq * 8
                            jobs.append((wo[:, j0:j0 + 8, n * 512:(n + 1) * 512],
                                         wdst[n * 2 + hf, :, qq * 4096:(qq + 1) * 4096]))
            engs = ["act", "dve"]
            if stage < 1:
                jobs = []
            NSL = 6

            def p0_load(idx):
                if idx < len(jobs):
                    s_ = idx % NSL
                    S.dma(st32[s_][:], jobs[idx][0], [], [B(f"st32_{s_}")], B(f"st32_{s_}"))

            for idx in range(NSL):
                p0_load(idx)
            for idx, (src, dst) in enumerate(jobs):
                s = idx % NSL
                b32 = B(f"st32_{s}")
                b16 = B(f"st16_{s}")
                en = engs[idx % 2]
                src_flat = st32[s][:].rearrange("p k c -> p (k c)")
                if en == "act":
                    S.op("act", lambda e, s=s, sf=src_flat: e.activation(out=st16[s][:], in_=sf, func=AF.Copy), [b32], [b16])
                else:
                    S.op(en, lambda e, s=s, sf=src_flat: e.tensor_copy(out=st16[s][:], in_=sf), [b32], [b16])
                S.dma(dst, st16[s][:], [b16], [B("wscr")], b16)
                p0_load(idx + NSL)
            S.mute = False
            S.barrier()

        wgl = E(nc.sbuf_tensor("wgl", [128, 16, 32], BF16))
        W2h = E(nc.sbuf_tensor("W2h", [33, 2048], BF16))
        W2l = E(nc.sbuf_tensor("W2l", [33, 2048], BF16))
        glh = E(nc.sbuf_tensor("glh", [33, 512], BF16))
        gll = E(nc.sbuf_tensor("gll", [33, 512], BF16))
        with ExitStack() as es0:
            S.mute = ('c' in KSKIP)
            E0 = es0.enter_context
            wgl32 = E0(nc.sbuf_tensor("wgl32", [128, 16, 32], F32))
            W2 = E0(nc.sbuf_tensor("W2", [33, 2048], F32))
            Bw = B("wgl32")
            S.dma(wgl32[:], w_in_b.rearrange("(k p) n -> p k n", p=128)[:, :, 10240:10272], [], [Bw], Bw)
            S.op("dve", lambda e: e.tensor_copy(out=wgl[:], in_=wgl32[:]), [Bw], [Bc])
            S.op("dve", lambda e: e.memset(W2[:], 0.0), [], [Bc])
            S.op("dve", lambda e: e.memset(glh[:], 1.0), [], [B("glT")])
            S.op("dve", lambda e: e.memset(gll[:], 0.0), [], [B("glT")])
            S.dma(W2[0:16, 0:1024], w_g2[0, :, :], [], [Bc], Bc)
            S.dma(W2[16:32, 1024:2048], w_g2[1, :, :], [], [Bc], Bc)
            S.dma(W2[32:33, :], b_g[:, :], [], [Bc], Bc)
            S.op("dve", lambda e: e.tensor_copy(out=W2h[:], in_=W2[:]), [Bc], [B("W2h")])
            S.op("dve", lambda e: e.tensor_tensor(out=W2l[:], in0=W2[:], in1=W2h[:], op=ALU.subtract), [Bc, B("W2h")], [B("W2l")])
            S.mute = False
            S.barrier()

        with ExitStack() as es1:
            E1 = es1.enter_context
            WsTb = E1(nc.sbuf_tensor("WsTb", [128, 16, 128], BF16))
            Bias = E1(nc.sbuf_tensor("Bias", [128, 32, 128], F32))
            with ExitStack() as es0:
                S.mute = ('d' in KSKIP)
                E0 = es0.enter_context
                WsTf = E0(nc.sbuf_tensor("WsTf", [128, 16, 128], F32))
                WsTl = E0(nc.sbuf_tensor("WsTl", [128, 16, 128], BF16))
                onesb = E0(nc.sbuf_tensor("onesb", [128, 128], BF16))
                Rb = E0(nc.sbuf_tensor("Rb", [128, 16, 128], F32))
                bsb = E0(nc.sbuf_tensor("bsb", [128, 16 * 128], F32))
                Bs = B("spsetup")
                S.dma(WsTf[:].rearrange("p a b -> p (a b)"), wsT_in[:, :], [], [Bs], Bs)
                S.dma(bsb[:], b_s[0:1, :].broadcast_to([128, 2048]), [], [B("bsb")], B("bsb"))
                S.op("dve", lambda e: e.memset(onesb[:], 1.0), [], [B("onesb")])
                S.op("dve", lambda e: e.tensor_copy(out=WsTb[:], in_=WsTf[:]), [Bs], [B("WsTb")])
                S.op("dve", lambda e: e.tensor_tensor(out=WsTl[:], in0=WsTf[:], in1=WsTb[:], op=ALU.subtract), [Bs, B("WsTb")], [B("WsTl")])
                for g4 in range(4):
                    pt, pb_ = nps()
                    for gg in range(4):
                        g = g4 * 4 + gg
                        S.op("pe", lambda e, g=g, gg=gg, pt=pt: e.matmul(pt[:, gg * 128:(gg + 1) * 128], lhsT=onesb[:], rhs=WsTb[:, g, :],
                                                                  start=True, stop=False), [B("onesb"), B("WsTb"), B("WsTl")], [pb_], inc=False)
                        S.op("pe", lambda e, g=g, gg=gg, pt=pt: e.matmul(pt[:, gg * 128:(gg + 1) * 128], lhsT=onesb[:], rhs=WsTl[:, g, :],
                                                                  start=False, stop=True), [B("onesb"), B("WsTb"), B("WsTl")], [pb_], inc=(gg == 3))
                    S.op("dve", lambda e, g4=g4, pt=pt: e.tensor_copy(out=Rb[:, g4 * 4:(g4 + 1) * 4, :].rearrange("p a b -> p (a b)"), in_=pt[:]),
                         [pb_], [B("Rb")])
                for j in range(32 if 'w' not in KSKIP else 0):
                    g = j // 2
                    S.op("dve", lambda e, j=j, g=g: e.scalar_tensor_tensor(out=Bias[:, j, :], in0=Rb[:, g, :], scalar=lnvb[:, j:j + 1],
                                                                         in1=bsb[:, g * 128:(g + 1) * 128], op0=ALU.mult, op1=ALU.add),
                         [B("Rb"), B("bsb"), Bc], [B("Bias")])
                S.mute = False
                S.barrier()

            bigA = E1(nc.sbuf_tensor("bigA", [128, 4, DI], BF16))
            xT = E1(nc.sbuf_tensor("xT", [128, 16, 512], BF16))
            pbuf = E1(nc.sbuf_tensor("pbuf", [128, 32, 512], BF16))
            wsl = [E1(nc.sbuf_tensor(f"wsl{i}", [128, 16, 512], BF16)) for i in range(2)]
            xin = [E1(nc.sbuf_tensor(f"xin{i}", [128, D], F32)) for i in range(2)]
            xb = [E1(nc.sbuf_tensor(f"xb{i}", [128, D], BF16)) for i in range(2)]
            tmpb = [E1(nc.sbuf_tensor(f"tmpb{i}", [128, 512], BF16)) for i in range(2)]
            tmpf = [E1(nc.sbuf_tensor(f"tmpf{i}", [128, 512], F32)) for i in range(2)]
            stg = [E1(nc.sbuf_tensor(f"stg{i}", [128, 4, 512], BF16)) for i in range(2)]
            stf = [E1(nc.sbuf_tensor(f"stf{i}", [128, 512], F32)) for i in range(2)]
            sps = [E1(nc.sbuf_tensor(f"sps{i}", [128, 2, 512], BF16)) for i in range(2)]
            stats = E1(nc.sbuf_tensor("stats", [128, 4, 8, 6], F32))
            mv = E1(nc.sbuf_tensor("mv", [128, 4, 2], F32))
            rstd = E1(nc.sbuf_tensor("rstd", [128, 4], F32))

            wlist = []
            for g in range(8):
                wlist.append(wA_in_s[g])
                wlist.append(wA_in_s[16 + g])
            for n in range(8):
                wlist.append(wA_in_s[8 + n])
            for n in range(8):
                wlist.append(wA_out_s[n])
            wseq = []
            for ti in range(NT):
                wseq.extend(wlist)
                is_own = T0 <= ti < T1
                for g in list(range(0, 4)) + list(range(12, 20)):
                    if is_own or g in (2, 3):
                        wseq.append(wB_in_s[g])
                for g in range(2, 12):
                    wseq.append(wB_in_s[g])
            wstate = {"issued": 0, "used": 0}
            total_w = len(wseq)

            def w_issue():
                k = wstate["issued"]
                if k >= total_w:
                    return
                s = k % 2
                bw = B(f"wsl{s}")
                S.dma(wsl[s][:].rearrange("p k c -> p (k c)"), wseq[k], [B("wscr")], [bw], bw)
                wstate["issued"] = k + 1

            def w_next():
                k = wstate["used"]
                while wstate["issued"] <= k:
                    w_issue()
                wstate["used"] = k + 1
                return wsl[k % 2], B(f"wsl{k % 2}")

            def w_done():
                w_issue()

            if stage >= 2:
                w_issue()
                w_issue()
            rr = {"i": 0}

            def alt(lst):
                rr["i"] += 1
                return rr["i"] % len(lst)

            def transpose_block(src_bf, srcB, b, affine):
                for half in range(2):
                    pt, pb_ = nps()
                    ptb = pt[:].bitcast(BF16)
                    for kk in range(8):
                        k = half * 8 + kk
                        S.op("pe", lambda e, k=k, kk=kk, ptb=ptb: e.transpose(out=ptb[:, kk * 128:(kk + 1) * 128], in_=src_bf[:, k * 128:(k + 1) * 128],
                                                                               identity=identb[:]), [srcB, Bc], [pb_], inc=(kk == 7))
                    if not affine:
                        S.op("act", lambda e, half=half, ptb=ptb: e.activation(out=xT[:, half * 8:(half + 1) * 8, b * 128:(b + 1) * 128],
                                                                             in_=ptb.rearrange("p (a c) -> p a c", a=8), func=AF.Copy),
                             [pb_], [B(f"xT{b}")])
                    else:
                        for kk in range(8):
                            k = half * 8 + kk
                            S.op("dve", lambda e, k=k, kk=kk, ptb=ptb: e.tensor_scalar(out=xT[:, k, b * 128:(b + 1) * 128], in0=ptb[:, kk * 128:(kk + 1) * 128],
                                                                                scalar1=l0g[:, k:k + 1], scalar2=l0b[:, k:k + 1], op0=ALU.mult, op1=ALU.add),
                                 [pb_, Bc], [B(f"xT{b}")])

            xTB = [B(f"xT{b}") for b in range(4)]
            bigB = [B(f"bigA{b}") for b in range(4)]

            def x_load(ti, b):
                s_ = b % 2
                bx = B(f"xin{s_}")
                S.dma(xin[s_][:], xw[(ti * 4 + b) * 128:(ti * 4 + b + 1) * 128, :], [], [bx], bx)

            def x_cast(b):
                s_ = b % 2
                if b % 2 == 0:
                    S.op("act", lambda e, s_=s_: e.activation(out=xb[s_][:], in_=xin[s_][:], func=AF.Copy), [B(f"xin{s_}")], [B(f"xb{s_}")])
                else:
                    S.op("dve", lambda e, s_=s_: e.tensor_copy(out=xb[s_][:], in_=xin[s_][:]), [B(f"xin{s_}")], [B(f"xb{s_}")])

            for i in range(NT if stage >= 2 else 0):
                if i == 0:
                    x_load(0, 0)
                    x_load(0, 1)
                    x_cast(0)
                    x_cast(1)
                    x_load(0, 2)
                    x_load(0, 3)
                transpose_block(xb[0], B("xb0"), 0, False)
                transpose_block(xb[1], B("xb1"), 1, False)
                x_cast(2)
                x_cast(3)
                transpose_block(xb[0], B("xb0"), 2, False)
                transpose_block(xb[1], B("xb1"), 3, False)
                if i + 1 < NT:
                    x_load(i + 1, 0)
                    x_load(i + 1, 1)
                for g in range(8):
                    for which in range(2):
                        W, WB = w_next()
                        for jj in range(4):
                            j = g * 4 + jj
                            pt, pb_ = nps()
                            for k in range(16):
                                S.op("pe", lambda e, k=k, jj=jj, W=W, pt=pt: e.matmul(pt[:], lhsT=W[:, k, jj * 128:(jj + 1) * 128], rhs=xT[:, k, :],
                                                                                  start=(k == 0), stop=(k == 15)), [WB] + xTB, [pb_], inc=(k == 15))
                            if which == 0:
                                S.op("act", lambda e, j=j, pt=pt: e.activation(out=pbuf[:, j, :], in_=pt[:], func=AF.Gelu_apprx_tanh), [pb_], [B(f"p{j}")])
                            else:
                                t = alt(tmpb)
                                S.op("act", lambda e, t=t, pt=pt: e.activation(out=tmpb[t][:], in_=pt[:], func=AF.Silu), [pb_], [B(f"tmpb{t}")])
                                S.op("pool", lambda e, t=t, j=j: e.tensor_tensor(out=pbuf[:, j, :], in0=pbuf[:, j, :], in1=tmpb[t][:], op=ALU.mult),
                                     [B(f"tmpb{t}")], [B(f"p{j}")])
                        w_done()
                for n in range(8):
                    W, WB = w_next()
                    for b in range(4):
                        pt, pb_ = nps()
                        for k in range(16):
                            S.op("pe", lambda e, k=k, b=b, W=W, pt=pt: e.matmul(pt[:], lhsT=xT[:, k, b * 128:(b + 1) * 128], rhs=W[:, k, :],
                                                                          start=(k == 0), stop=(k == 15)), [WB, xTB[b]], [pb_], inc=(k == 15))
                        S.op("act", lambda e, b=b, n=n, pt=pt: e.activation(out=bigA[:, b, n * 512:(n + 1) * 512], in_=pt[:], func=AF.Gelu_apprx_tanh),
                             [pb_], [bigB[b]])
                        S.op("dve", lambda e, b=b, n=n: e.bn_stats(out=stats[:, b, n, :], in_=bigA[:, b, n * 512:(n + 1) * 512]), [bigB[b]], [B(f"stats{b}")])
                    w_done()
                for b in range(4):
                    S.op("dve", lambda e, b=b: e.bn_aggr(out=mv[:, b, :], in_=stats[:, b, :, :].rearrange("p a c -> p (a c)")), [B(f"stats{b}")], [B(f"mv{b}")])
                    S.op("act", lambda e, b=b: e.activation(out=rstd[:, b:b + 1], in_=mv[:, b, 1:2], func=AF.Ln, bias=epsc[:, 0:1]), [B(f"mv{b}"), Bc], [B(f"rstd{b}")])
                    S.op("act", lambda e, b=b: e.activation(out=rstd[:, b:b + 1], in_=rstd[:, b:b + 1], func=AF.Exp, scale=-0.5), [], [B(f"rstd{b}")])
                    S.op("dve", lambda e, b=b: e.tensor_scalar(out=bigA[:, b, :], in0=bigA[:, b, :], scalar1=mv[:, b, 0:1], scalar2=rstd[:, b:b + 1],
                                                               op0=ALU.subtract, op1=ALU.mult), [B(f"mv{b}"), B(f"rstd{b}")], [bigB[b]])
                for j in range(32):
                    g = j // 2
                    pt, pb_ = nps()
                    for b in range(4):
                        S.op("pe", lambda e, b=b, j=j, g=g, pt=pt: e.matmul(pt[:, b * 128:(b + 1) * 128], lhsT=bigA[:, b, j * 128:(j + 1) * 128], rhs=WsTb[:, g, :],
                                                                      start=True, stop=True), [bigB[b], B("WsTb")], [pb_], inc=(b == 3))
                    t = alt(tmpf)
                    S.op("dve", lambda e, t=t, j=j, pt=pt: e.scalar_tensor_tensor(out=tmpf[t][:].rearrange("p (a c) -> p a c", a=4),
                                                                             in0=pt[:].rearrange("p (a c) -> p a c", a=4), scalar=lnvg[:, j:j + 1],
                                                                             in1=Bias[:, j:j + 1, :].broadcast_to([128, 4, 128]), op0=ALU.mult, op1=ALU.add),
                         [pb_, B("Bias"), Bc], [B(f"tmpf{t}")])
                    S.op("pool", lambda e, t=t, j=j: e.tensor_tensor(out=pbuf[:, j, :], in0=pbuf[:, j, :], in1=tmpf[t][:], op=ALU.mult),
                         [B(f"tmpf{t}")], [B(f"p{j}")])
                rA = [bigA[:, b, :].bitcast(F32) for b in range(4)]
                for b in range(4):
                    S.dma(rA[b], xw[(i * 4 + b) * 128:(i * 4 + b + 1) * 128, :], [], [bigB[b]], bigB[b])
                pB = [B(f"p{j}") for j in range(32)]
                for n in range(4):
                    pts = [nps() for _ in range(4)]
                    for hf in range(2):
                        W, WB = w_next()
                        for b in range(4):
                            pt, pb_ = pts[b]
                            for jj in range(16):
                                j = hf * 16 + jj
                                S.op("pe", lambda e, b=b, j=j, jj=jj, W=W, pt=pt, hf=hf: e.matmul(pt[:], lhsT=pbuf[:, j, b * 128:(b + 1) * 128], rhs=W[:, jj, :],
                                                                                          start=(hf == 0 and jj == 0), stop=(hf == 1 and jj == 15)),
                                     [WB] + pB, [pb_], inc=(jj == 15))
                        w_done()
                    for b in range(4):
                        pt, pb_ = pts[b]
                        S.op("dve", lambda e, b=b, n=n, pt=pt: e.scalar_tensor_tensor(out=rA[b][:, n * 512:(n + 1) * 512], in0=rA[b][:, n * 512:(n + 1) * 512],
                                                                                 scalar=ALPHA, in1=pt[:], op0=ALU.mult, op1=ALU.add), [pb_], [bigB[b]])
                        S.op("dve", lambda e, b=b, n=n: e.bn_stats(out=stats[:, b, n, :], in_=rA[b][:, n * 512:(n + 1) * 512]), [bigB[b]], [B(f"stats{b}")])
                for b in range(4):
                    S.op("dve", lambda e, b=b: e.bn_aggr(out=mv[:, b, :], in_=stats[:, b, 0:4, :].rearrange("p a c -> p (a c)")), [B(f"stats{b}")], [B(f"mv{b}")])
                    S.op("act", lambda e, b=b: e.activation(out=rstd[:, b:b + 1], in_=mv[:, b, 1:2], func=AF.Ln, bias=epsc[:, 0:1]), [B(f"mv{b}"), Bc], [B(f"rstd{b}")])
                    S.op("act", lambda e, b=b: e.activation(out=rstd[:, b:b + 1], in_=rstd[:, b:b + 1], func=AF.Exp, scale=-0.5), [], [B(f"rstd{b}")])
                    S.op("dve", lambda e, b=b: e.tensor_scalar(out=rA[b], in0=rA[b], scalar1=mv[:, b, 0:1], scalar2=rstd[:, b:b + 1],
                                                               op0=ALU.subtract, op1=ALU.mult), [B(f"mv{b}"), B(f"rstd{b}")], [bigB[b]])
                    S.dma(x1h_s[(i * 4 + b) * 128:(i * 4 + b + 1) * 128, :], rA[b], [bigB[b]], [B(f"x1h{i * 4 + b}")], bigB[b])
                    s = b % 2
                    S.op("act", lambda e, b=b, s=s: e.activation(out=xb[s][:], in_=rA[b], func=AF.Copy), [bigB[b]], [B(f"xb{s}")])
                    transpose_block(xb[s], B(f"xb{s}"), b, True)
                own_tile = T0 <= i < T1
                for gi, g in enumerate(list(range(0, 4)) + list(range(12, 20))):
                    if not own_tile and g not in (2, 3):
                        continue
                    W, WB = w_next()
                    sg = alt(stg)
                    bsg = B(f"stg{sg}")
                    for jj in range(4):
                        pt, pb_ = nps()
                        for k in range(16):
                            S.op("pe", lambda e, k=k, jj=jj, W=W, pt=pt: e.matmul(pt[:], lhsT=W[:, k, jj * 128:(jj + 1) * 128], rhs=xT[:, k, :],
                                                                              start=(k == 0), stop=(k == 15)), [WB] + xTB, [pb_], inc=(k == 15))
                        if g < 2:
                            S.op("act", lambda e, jj=jj, sg=sg, pt=pt: e.activation(out=stg[sg][:, jj, :], in_=pt[:], func=AF.Copy, scale=0.0625), [pb_], [bsg])
                        elif g < 4:
                            S.op("act", lambda e, jj=jj, sg=sg, pt=pt: e.activation(out=stg[sg][:, jj, :], in_=pt[:], func=AF.Copy), [pb_], [bsg])
                        else:
                            S.op("act", lambda e, jj=jj, sg=sg, pt=pt: e.activation(out=stg[sg][:, jj, :], in_=pt[:], func=AF.Silu), [pb_], [bsg])
                            jz = (g - 12) * 4 + jj
                            S.op("dve", lambda e, jj=jj, sg=sg, jz=jz: e.tensor_scalar(out=stg[sg][:, jj, :], in0=stg[sg][:, jj, :], scalar1=gncol[:, jz:jz + 1],
                                                                                     scalar2=None, op0=ALU.mult), [Bc], [bsg])
                    w_done()
                    for b in range(4):
                        ch = i * 4 + b
                        if g < 2:
                            dst = qT_s[ch, :, g * 4:(g + 1) * 4, :]
                        elif g < 4:
                            dst = kT_s[ch, :, (g - 2) * 4:(g - 1) * 4, :]
                        else:
                            dst = zT_s[ch, :, (g - 12) * 4:(g - 11) * 4, :]
                        S.dma(dst, stg[sg][:, :, b * 128:(b + 1) * 128], [bsg], [B(f"fm{ch}")], bsg)
                for g in range(2, 12):
                    W, WB = w_next()
                    sg = alt(stg)
                    bsg = B(f"stg{sg}")
                    for b in range(4):
                        pt, pb_ = nps()
                        for k in range(16):
                            S.op("pe", lambda e, k=k, b=b, W=W, pt=pt: e.matmul(pt[:], lhsT=xT[:, k, b * 128:(b + 1) * 128], rhs=W[:, k, :],
                                                                          start=(k == 0), stop=(k == 15)), [WB, xTB[b]], [pb_], inc=(k == 15))
                        S.op("act", lambda e, b=b, sg=sg, pt=pt: e.activation(out=stg[sg][:, b, :], in_=pt[:], func=AF.Copy), [pb_], [bsg])
                    w_done()
                    if g < 4:
                        dst = ktok_s[i * 512:(i + 1) * 512, (g - 2) * 512:(g - 1) * 512]
                    else:
                        dst = v_s[i * 512:(i + 1) * 512, (g - 4) * 512:(g - 3) * 512]
                    S.dma(dst.rearrange("(b p) c -> p b c", p=128), stg[sg][:], [bsg], [B(f"tm{i}")], bsg)
                pt, pb_ = nps()
                for k in range(16):
                    S.op("pe", lambda e, k=k, pt=pt: e.matmul(pt[0:32, :], lhsT=wgl[:, k, :], rhs=xT[:, k, :], start=(k == 0), stop=(k == 15)),
                         [Bc] + xTB, [pb_], inc=(k == 15))
                S.op("dve", lambda e, pt=pt: e.tensor_copy(out=glh[0:32, :], in_=pt[0:32, :]), [pb_], [B("glT")])
                S.op("dve", lambda e, pt=pt: e.tensor_tensor(out=gll[0:32, :], in0=pt[0:32, :], in1=glh[0:32, :], op=ALU.subtract), [pb_], [B("glT")])
                for b in range(4):
                    for c4 in range(4):
                        pt, pb_ = nps()
                        bs_ = slice(b * 128, (b + 1) * 128)
                        cs_ = slice(c4 * 512, (c4 + 1) * 512)
                        S.op("pe", lambda e, bs_=bs_, cs_=cs_, pt=pt: e.matmul(pt[:], lhsT=glh[0:33, bs_], rhs=W2h[0:33, cs_], start=True, stop=False),
                             [B("glT"), B("W2h"), B("W2l")], [pb_], inc=False)
                        S.op("pe", lambda e, bs_=bs_, cs_=cs_, pt=pt: e.matmul(pt[:], lhsT=gll[0:33, bs_], rhs=W2h[0:33, cs_], start=False, stop=False),
                             [B("glT"), B("W2h"), B("W2l")], [pb_], inc=False)
                        S.op("pe", lambda e, bs_=bs_, cs_=cs_, pt=pt: e.matmul(pt[:], lhsT=glh[0:33, bs_], rhs=W2l[0:33, cs_], start=False, stop=True),
                             [B("glT"), B("W2h"), B("W2l")], [pb_])
                        t = alt(tmpf)
                        sf = alt(stf)
                        sq = alt(sps)
                        S.op("act", lambda e, t=t, pt=pt: e.activation(out=tmpf[t][:], in_=pt[:], func=AF.Exp, scale=-1.0), [pb_], [B(f"tmpf{t}")])
                        S.op("act", lambda e, t=t, sf=sf: e.activation(out=stf[sf][:], in_=tmpf[t][:], func=AF.Ln, bias=1.0), [B(f"tmpf{t}")], [B(f"stf{sf}")])
                        S.op("pool", lambda e, sf=sf, sq=sq: e.tensor_copy(out=sps[sq][:, 0, :], in_=stf[sf][:]), [B(f"stf{sf}")], [B(f"sps{sq}")])
                        S.op("dve", lambda e, sf=sf, sq=sq: e.tensor_tensor(out=sps[sq][:, 1, :], in0=stf[sf][:], in1=sps[sq][:, 0, :], op=ALU.subtract),
                             [B(f"stf{sf}")], [B(f"sps{sq}")])
                        d_ = c4 // 2
                        r_ = slice((i * 4 + b) * 128, (i * 4 + b + 1) * 128)
                        dst = sp_s[r_, d_ * 2048:(d_ + 1) * 2048].rearrange("p (a c) -> p a c", a=2)[:, :, (c4 % 2) * 512:(c4 % 2 + 1) * 512]
                        S.dma(dst, sps[sq][:], [B(f"sps{sq}")], [B(f"sp{i * 4 + b}")], B(f"sps{sq}"))
                if i + 1 < NT:
                    x_cast(0)
                    x_cast(1)
                    x_load(i + 1, 2)
                    x_load(i + 1, 3)
            S.barrier()

        prep_banks = [0, 1, 2, 3]
        head_banks = [4, 5, 6, 7]
        pst2 = {"p": 0, "h": 0}

        def nps_p():
            i = prep_banks[pst2["p"] % 4]
            pst2["p"] += 1
            return psum[i], psb[i]

        def nps_h():
            i = head_banks[pst2["h"] % 4]
            pst2["h"] += 1
            return psum[i], psb[i]

        with ExitStack() as es2:
            E2 = es2.enter_context
            Sst = E2(nc.sbuf_tensor("Sst", [128, 8, 1024], F32))
            Sbf = E2(nc.sbuf_tensor("Sbf", [128, 8, 1024], BF16))
            sp_t = E2(nc.sbuf_tensor("sp_t", [128, 2048], BF16))
            qT_t = E2(nc.sbuf_tensor("qT_t", [128, 8, 128], BF16))
            kT_t = E2(nc.sbuf_tensor("kT_t", [128, 8, 128], BF16))
            ktok_t = E2(nc.sbuf_tensor("ktok_t", [128, 1024], BF16))
            E4 = E2(nc.sbuf_tensor("E4", [128, 1024], F32))
            Ea = E2(nc.sbuf_tensor("Ea", [128, 3, 8, 128], F32))
            Kt = E2(nc.sbuf_tensor("Kt", [128, 1024], BF16))
            v_t = [E2(nc.sbuf_tensor(f"v_t{i}", [128, DI], BF16)) for i in range(2)]
            Kd = [E2(nc.sbuf_tensor(f"Kd{i}", [128, 1024], BF16)) for i in range(2)]
            Pa = [E2(nc.sbuf_tensor(f"Pa{i}", [128, 3, 8, 128], BF16)) for i in range(2)]
            scp = [E2(nc.sbuf_tensor(f"scp{i}", [128, 24], F32)) for i in range(2)]
            zT_t = E2(nc.sbuf_tensor("zT_t", [128, 32, 128], BF16))
            of_t = [E2(nc.sbuf_tensor(f"of_t{i}", [128, 1024], F32)) for i in range(2)]
            osb = [E2(nc.sbuf_tensor(f"osb{i}", [128, 1024], F32)) for i in range(2)]
            junk = E2(nc.sbuf_tensor("junk", [128, 1024], BF16))
            aTs = [E2(nc.sbuf_tensor(f"aTs{i}", [128, 4, 128], BF16)) for i in range(2)]
            sc = E2(nc.sbuf_tensor("sc", [128, 8], F32))
            onb = E2(nc.sbuf_tensor("onb", [128, DI], BF16))
            yT = [E2(nc.sbuf_tensor(f"yT{i}", [128, 32, 128], BF16)) for i in range(2)]

            def prep(n, d, do_out, p):
                nn = n + 1 if d == 0 else n - 1
                has_next = 0 <= nn < NCH
                last_col = 127 if d == 0 else 0
                Bsp, Bq, Bk, Bkt, Bv = B("sp_t"), B("qT_t"), B("kT_t"), B("ktok_t"), B(f"v_t{p}")
                r0 = n * 128
                S.dma(sp_t[:], sp_s[r0:r0 + 128, d * 2048:(d + 1) * 2048], [B(f"sp{n}")], [Bsp], Bsp)
                S.dma(ktok_t[:], ktok_s[r0:r0 + 128, :], [B(f"tm{n // 4}")], [Bkt], Bkt)
                S.dma(v_t[p][:], v_s[r0:r0 + 128, :], [B(f"tm{n // 4}")], [Bv], Bv)
                if do_out:
                    S.dma(qT_t[:], qT_s[n], [B(f"fm{n}")], [Bq], Bq)
                    S.dma(kT_t[:], kT_s[n], [B(f"fm{n}")], [Bk], Bk)
                gts = []
                for half in range(2):
                    pt, pb_ = nps_p()
                    for c4 in range(4):
                        dc = half * 4 + c4
                        S.op("pe", lambda e, dc=dc, c4=c4, pt=pt: e.matmul(pt[:, c4 * 128:(c4 + 1) * 128], lhsT=sp_t[:, dc * 128:(dc + 1) * 128], rhs=tri[d],
                                                                      start=True, stop=False), [Bsp, Bc], [pb_], inc=False)
                        S.op("pe", lambda e, dc=dc, c4=c4, pt=pt: e.matmul(pt[:, c4 * 128:(c4 + 1) * 128], lhsT=sp_t[:, 1024 + dc * 128:1024 + (dc + 1) * 128], rhs=tri[d],
                                                                      start=False, stop=True), [Bsp, Bc], [pb_], inc=(c4 == 3))
                    gts.append((pt, pb_))
                grs = []
                for half in range(2):
                    pt, pb_ = nps_p()
                    S.op("pe", lambda e, half=half, pt=pt: e.matmul(pt[:], lhsT=triR[d], rhs=sp_t[:, half * 512:(half + 1) * 512], start=True, stop=False),
                         [Bsp, Bc], [pb_], inc=False)
                    S.op("pe", lambda e, half=half, pt=pt: e.matmul(pt[:], lhsT=triR[d], rhs=sp_t[:, 1024 + half * 512:1024 + (half + 1) * 512], start=False, stop=True),
                         [Bsp, Bc], [pb_])
                    grs.append((pt, pb_))
                Bsc = B(f"scp{p}")
                scq = scp[p]
                for half in range(2):
                    pt, pb_ = gts[half]
                    pv = pt[:].rearrange("p (a c) -> p a c", a=4)
                    S.op("act", lambda e, half=half, pv=pv: e.activation(out=scq[:, 16 + half * 4:20 + half * 4], in_=pv[:, :, last_col], func=AF.Exp), [pb_], [Bsc])
                BE4, BKd, BKt = B("E4"), B(f"Kd{p}"), B("Kt")
                if has_next:
                    kcol = keep[d][:, nn:nn + 1]
                    S.op("dve", lambda e, kcol=kcol: e.tensor_scalar(out=scq[:, 16:24], in0=scq[:, 16:24], scalar1=kcol, scalar2=None, op0=ALU.mult), [Bc], [Bsc])
                    for half in range(2):
                        pt, pb_ = grs[half]
                        S.op("act", lambda e, half=half, pt=pt: e.activation(out=E4[:, half * 512:(half + 1) * 512], in_=pt[:], func=AF.Exp), [pb_], [BE4])
                    S.op("dve", lambda e, kcol=kcol: e.scalar_tensor_tensor(out=Kd[p][:], in0=ktok_t[:], scalar=kcol, in1=E4[:], op0=ALU.mult, op1=ALU.mult),
                         [Bkt, BE4, Bc], [BKd])
                BEa = B("Ea")
                if do_out:
                    for half in range(2):
                        pt, pb_ = gts[half]
                        S.op("act", lambda e, half=half, pt=pt: e.activation(out=Ea[:, 2, half * 4:(half + 1) * 4, :].rearrange("p a c -> p (a c)"), in_=pt[:], func=AF.Exp),
                             [pb_], [BEa])
                    gcs = []
                    for half in range(2):
                        pt, pb_ = nps_p()
                        for c4 in range(4):
                            dc = half * 4 + c4
                            S.op("pe", lambda e, dc=dc, c4=c4, pt=pt: e.matmul(pt[:, c4 * 128:(c4 + 1) * 128], lhsT=sp_t[:, dc * 128:(dc + 1) * 128], rhs=triC[d],
                                                                          start=True, stop=False), [Bsp, Bc], [pb_], inc=False)
                            S.op("pe", lambda e, dc=dc, c4=c4, pt=pt: e.matmul(pt[:, c4 * 128:(c4 + 1) * 128], lhsT=sp_t[:, 1024 + dc * 128:1024 + (dc + 1) * 128], rhs=triC[d],
                                                                          start=False, stop=True), [Bsp, Bc], [pb_], inc=(c4 == 3))
                        gcs.append((pt, pb_))
                    for half in range(2):
                        pc, pcb = gcs[half]
                        S.op("act", lambda e, half=half, pc=pc: e.activation(out=Ea[:, 0, half * 4:(half + 1) * 4, :].rearrange("p a c -> p (a c)"), in_=pc[:], func=AF.Exp),
                             [pcb], [BEa])
                        S.op("act", lambda e, half=half, pc=pc: e.activation(out=Ea[:, 1, half * 4:(half + 1) * 4, :].rearrange("p a c -> p (a c)"), in_=pc[:], func=AF.Exp,
                                                                          scale=-1.0), [pcb], [BEa])
                    BP0, BP1, BP2 = B(f"Pa{p}_0"), B(f"Pa{p}_1"), B(f"Pa{p}_2")
                    S.op("dve", lambda e: e.tensor_tensor(out=Pa[p][:, 0, :, :], in0=qT_t[:], in1=Ea[:, 0, :, :], op=ALU.mult), [Bq, BEa], [BP0])
                    S.op("dve", lambda e: e.tensor_tensor(out=Pa[p][:, 1, :, :], in0=kT_t[:], in1=Ea[:, 1, :, :], op=ALU.mult), [Bk, BEa], [BP1])
                    S.op("dve", lambda e: e.tensor_tensor(out=Pa[p][:, 2, :, :], in0=qT_t[:], in1=Ea[:, 2, :, :], op=ALU.mult), [Bq, BEa], [BP2])
                    for h in range(4):
                        pa, pab = nps_p()
                        for c2 in range(2):
                            dc = 2 * h + c2
                            S.op("pe", lambda e, dc=dc, c2=c2, pa=pa: e.matmul(pa[:, 0:128], lhsT=Pa[p][:, 1, dc, :], rhs=Pa[p][:, 0, dc, :], start=(c2 == 0), stop=(c2 == 1)),
                                 [BP0, BP1], [pab], inc=(c2 == 1))
                        S.op("dve", lambda e, h=h, pa=pa: e.tensor_tensor(out=aTs[p][:, h, :], in0=pa[:, 0:128], in1=maskT[d], op=ALU.mult), [pab, Bc], [B(f"aTs{p}_{h}")])

            def heads(n, d, do_out, final, p):
                nn = n + 1 if d == 0 else n - 1
                has_next = 0 <= nn < NCH
                r0 = n * 128
                Bv, BKd, Bsc = B(f"v_t{p}"), B(f"Kd{p}"), B(f"scp{p}")
                vt, Pq, Kq, scq = v_t[p], Pa[p], Kd[p], scp[p]
                for h in range(4):
                    if do_out:
                        Ba = B(f"aTs{p}_{h}")
                        aT_ap = aTs[p][:, h, :]
                        oi = alt(osb)
                        Bo = B(f"osb{oi}")
                        if final:
                            Bof = B(f"of_t{oi}")
                            S.dma(of_t[oi][:], of_s[r0:r0 + 128, h * 1024:(h + 1) * 1024], [B(f"of{n}")], [Bof], Bof)
                        for half in range(2):
                            po, pob = nps_h()
                            c0 = h * 1024 + half * 512
                            S.op("pe", lambda e, aT_ap=aT_ap, c0=c0, po=po: e.matmul(po[:], lhsT=aT_ap, rhs=vt[:, c0:c0 + 512], start=True, stop=False),
                                 [Ba, Bv], [pob], inc=False)
                            for c2 in range(2):
                                dc = 2 * h + c2
                                S.op("pe", lambda e, dc=dc, c2=c2, half=half, po=po: e.matmul(po[:], lhsT=Pq[:, 2, dc, :], rhs=Sbf[:, dc, half * 512:(half + 1) * 512],
                                                                                       start=False, stop=(c2 == 1)), [B(f"Pa{p}_2"), B(f"Sbf{dc}")], [pob], inc=(c2 == 1))
                            if not final:
                                S.op("act", lambda e, oi=oi, half=half, po=po: e.activation(out=osb[oi][:, half * 512:(half + 1) * 512], in_=po[:], func=AF.Copy), [pob], [Bo])
                            else:
                                S.op("dve", lambda e, oi=oi, half=half, po=po: e.tensor_tensor(out=osb[oi][:, half * 512:(half + 1) * 512], in0=po[:],
                                                                                           in1=of_t[oi][:, half * 512:(half + 1) * 512], op=ALU.add), [pob, Bof], [Bo])
                        if not final:
                            S.dma(of_s[r0:r0 + 128, h * 1024:(h + 1) * 1024], osb[oi][:], [Bo], [B(f"of{n}")], Bo, q="act")
                        else:
                            S.op("act", lambda e, oi=oi, h=h: e.activation(out=junk[:], in_=osb[oi][:], func=AF.Square, accum_out=sc[:, h:h + 1]), [Bo], [B("junk"), B(f"ss{h}")])
                            S.op("act", lambda e, h=h: e.activation(out=sc[:, 4 + h:5 + h], in_=sc[:, h:h + 1], func=AF.Ln, bias=epsc[:, 1:2], scale=1.0 / 1024.0),
                                 [B(f"ss{h}"), Bc], [B(f"rs{h}")])
                            S.op("act", lambda e, h=h: e.activation(out=sc[:, 4 + h:5 + h], in_=sc[:, 4 + h:5 + h], func=AF.Exp, scale=-0.5),
                                 [], [B(f"rs{h}")])
                            S.op("act", lambda e, oi=oi, h=h: e.activation(out=onb[:, h * 1024:(h + 1) * 1024], in_=osb[oi][:], func=AF.Copy, scale=sc[:, 4 + h:5 + h]),
                                 [Bo, B(f"rs{h}")], [B("onb")])
                    if has_next:
                        for c2 in range(2):
                            dc = 2 * h + c2
                            for half in range(2):
                                pu, pub = nps_h()
                                c0 = h * 1024 + half * 512
                                S.op("pe", lambda e, dc=dc, c0=c0, pu=pu: e.matmul(pu[:], lhsT=Kq[:, dc * 128:(dc + 1) * 128], rhs=vt[:, c0:c0 + 512], start=True, stop=True),
                                     [BKd, Bv], [pub])
                                S.op("dve", lambda e, dc=dc, half=half, pu=pu: e.scalar_tensor_tensor(out=Sst[:, dc, half * 512:(half + 1) * 512],
                                                                                                 in0=Sst[:, dc, half * 512:(half + 1) * 512], scalar=scq[:, 16 + dc:17 + dc],
                                                                                                 in1=pu[:], op0=ALU.mult, op1=ALU.add), [pub, Bsc], [B(f"Sst{dc}")])
                            S.op("act", lambda e, dc=dc: e.activation(out=Sbf[:, dc, :], in_=Sst[:, dc, :], func=AF.Copy), [B(f"Sst{dc}")], [B(f"Sbf{dc}")])
                if do_out and final:
                    Bz = B("zT_t")
                    S.dma(zT_t[:], zT_s[n], [B(f"fm{n}")], [Bz], Bz)
                    yi = alt(yT)
                    ByT = B(f"yT{yi}")
                    for j4 in range(4):
                        pt, pb_ = nps_h()
                        ptb = pt[:].bitcast(BF16)
                        for jj in range(8):
                            j = j4 * 8 + jj
                            S.op("pe", lambda e, j=j, jj=jj, ptb=ptb: e.transpose(out=ptb[:, jj * 128:(jj + 1) * 128], in_=onb[:, j * 128:(j + 1) * 128], identity=identb[:]),
                                 [B("onb"), Bc], [pb_], inc=(jj == 7))
                        S.op("dve", lambda e, j4=j4, ptb=ptb, yi=yi: e.tensor_tensor(out=yT[yi][:, j4 * 8:(j4 + 1) * 8, :].rearrange("p a c -> p (a c)"), in0=ptb[:, :],
                                                                                 in1=zT_t[:, j4 * 8:(j4 + 1) * 8, :].rearrange("p a c -> p (a c)"), op=ALU.mult), [pb_, Bz], [ByT])
                    S.dma(yT_s[n], yT[yi][:], [ByT], [B(f"yTs{n}")], ByT)

            def run_scan(chunks, d, out_pred, final):
                S.op("dve", lambda e: e.memset(Sst[:], 0.0), [], [B(f"Sst{dc}") for dc in range(8)])
                S.op("pool", lambda e: e.memset(Sbf[:], 0.0), [], [B(f"Sbf{dc}") for dc in range(8)])
                if not chunks:
                    return
                prep(chunks[0], d, out_pred(chunks[0]), 0)
                for i, n in enumerate(chunks):
                    if i + 1 < len(chunks):
                        prep(chunks[i + 1], d, out_pred(chunks[i + 1]), (i + 1) % 2)
                    heads(n, d, out_pred(n), final, i % 2)

            run_scan(list(range(0, T1 * 4)) if stage >= 3 else [], 0, lambda n: n >= T0 * 4, False)
            S.barrier()
            run_scan(list(range(NCH - 1, T0 * 4 - 1, -1)) if stage >= 4 else [], 1, lambda n: n < T1 * 4, True)
            S.barrier()

        with ExitStack() as es4:
            E4_ = es4.enter_context
            yTt2 = [E4_(nc.sbuf_tensor(f"yTt{i}", [128, 4, 32, 128], BF16)) for i in range(1)]
            rr4 = E4_(nc.sbuf_tensor("rr4", [128, 4, D], F32))
            wo = [E4_(nc.sbuf_tensor(f"wo{i}", [128, 16, 512], BF16)) for i in range(5)]
            bcg = [E4_(nc.sbuf_tensor(f"bcg{i}", [128, D], F32)) for i in range(4)]
            st4 = E4_(nc.sbuf_tensor("st4", [128, 4, 4, 6], F32))
            mv4 = E4_(nc.sbuf_tensor("mv4", [128, 4, 2], F32))
            Bbc = B("bcg")
            S.dma(bcg[0][:], ln_g[0:1, :].broadcast_to([128, D]), [], [Bbc], Bbc)
            S.dma(bcg[1][:], ln_b[0:1, :].broadcast_to([128, D]), [], [Bbc], Bbc)
            S.dma(bcg[2][:], ln_g[1:2, :].broadcast_to([128, D]), [], [Bbc], Bbc)
            S.dma(bcg[3][:], ln_b[1:2, :].broadcast_to([128, D]), [], [Bbc], Bbc)
            wo_k = {"issued": 0, "used": 0}
            tiles4 = list(range(T0, T1)) if stage >= 5 else []
            tot4 = len(tiles4) * 8

            def wo_issue():
                k = wo_k["issued"]
                if k >= tot4:
                    return
                s_ = k % 5
                Bw = B(f"wo{s_}")
                S.dma(wo[s_][:].rearrange("p k c -> p (k c)"), wB_out_s[k % 8], [B("wscr")], [Bw], Bw)
                wo_k["issued"] = k + 1

            for _ in range(4):
                wo_issue()
            for ti4, i in enumerate(tiles4):
                yTt = yTt2[0]
                for b in range(4):
                    n = i * 4 + b
                    By = B(f"yTt0_{b}")
                    S.dma(yTt[:, b, :, :], yT_s[n], [B(f"yTs{n}")], [By], By)
                    Br = B(f"rr4{b}")
                    S.dma(rr4[:, b, :], x1h_s[n * 128:(n + 1) * 128, :], [B(f"x1h{n}")], [Br], Br)
                    S.op("pool", lambda e, b=b: e.tensor_tensor(out=rr4[:, b, :], in0=rr4[:, b, :], in1=bcg[0][:], op=ALU.mult), [Bbc], [Br])
                    S.op("pool", lambda e, b=b: e.tensor_tensor(out=rr4[:, b, :], in0=rr4[:, b, :], in1=bcg[1][:], op=ALU.add), [Bbc], [Br])
                for nt in range(4):
                    pts = [nps() for _ in range(4)]
                    for hf in range(2):
                        k = wo_k["used"]
                        while wo_k["issued"] <= k:
                            wo_issue()
                        wo_k["used"] = k + 1
                        s_ = k % 5
                        Bw = B(f"wo{s_}")
                        for b in range(4):
                            pt, pb_ = pts[b]
                            for jj in range(16):
                                j = hf * 16 + jj
                                S.op("pe", lambda e, b=b, j=j, jj=jj, s_=s_, pt=pt, hf=hf, yTt=yTt: e.matmul(pt[:], lhsT=yTt[:, b, j, :], rhs=wo[s_][:, jj, :],
                                                                                            start=(hf == 0 and jj == 0), stop=(hf == 1 and jj == 15)),
                                     [B(f"yTt0_{b}"), Bw], [pb_], inc=(jj == 15))
                        wo_issue()
                    for b in range(4):
                        pt, pb_ = pts[b]
                        Br = B(f"rr4{b}")
                        S.op("dve", lambda e, b=b, nt=nt, pt=pt: e.scalar_tensor_tensor(out=rr4[:, b, nt * 512:(nt + 1) * 512], in0=rr4[:, b, nt * 512:(nt + 1) * 512],
                                                                                   scalar=ALPHA, in1=pt[:], op0=ALU.mult, op1=ALU.add), [pb_], [Br])
                        S.op("dve", lambda e, b=b, nt=nt: e.bn_stats(out=st4[:, b, nt, :], in_=rr4[:, b, nt * 512:(nt + 1) * 512]), [Br], [B(f"st4{b}")])
                for b in range(4):
                    n = i * 4 + b
                    Br = B(f"rr4{b}")
                    Bm = B(f"mv4{b}")
                    S.op("dve", lambda e, b=b: e.bn_aggr(out=mv4[:, b, :], in_=st4[:, b, :, :].rearrange("p a c -> p (a c)")), [B(f"st4{b}")], [Bm])
                    S.op("act", lambda e, b=b: e.activation(out=mv4[:, b, 1:2], in_=mv4[:, b, 1:2], func=AF.Ln, bias=epsc[:, 0:1]), [Bc], [Bm])
                    S.op("act", lambda e, b=b: e.activation(out=mv4[:, b, 1:2], in_=mv4[:, b, 1:2], func=AF.Exp, scale=-0.5), [], [Bm])
                    S.op("dve", lambda e, b=b: e.tensor_scalar(out=rr4[:, b, :], in0=rr4[:, b, :], scalar1=mv4[:, b, 0:1], scalar2=mv4[:, b, 1:2], op0=ALU.subtract, op1=ALU.mult),
                         [Bm], [Br])
                    S.op("pool", lambda e, b=b: e.tensor_tensor(out=rr4[:, b, :], in0=rr4[:, b, :], in1=bcg[2][:], op=ALU.mult), [Bbc], [Br])
                    S.op("pool", lambda e, b=b: e.tensor_tensor(out=rr4[:, b, :], in0=rr4[:, b, :], in1=bcg[3][:], op=ALU.add), [Bbc], [Br])
                    o0 = n * 128 - T0 * 512
                    S.dma(yout[o0:o0 + 128, :], rr4[:, b, :], [Br], [B("yout")], Br)
            S.barrier()
        S.finish()
    return nc


def make_consts(NCH, keepf, keepb, colsT=None):
    c = np.zeros((128, 7 * 128 + 2 * NCH + 128 + 256), np.float32)
    s = np.arange(128)[:, None]
    t = np.arange(128)[None, :]
    c[:, 0:128] = np.eye(128, dtype=np.float32)
    c[:, 128:256] = np.where(s <= t, -1.0 / 16.0, 0.0)
    c[:, 256:384] = np.where(s > t, -1.0 / 16.0, 0.0)
    c[:, 384:512] = np.where(s >= t, -1.0 / 16.0, 0.0)
    c[:, 512:640] = np.where(s < t, -1.0 / 16.0, 0.0)
    c[:, 640:768] = np.where(s <= t, 1.0, 0.0)
    c[:, 768:896] = np.where(s > t, 1.0, 0.0)
    c[:, 896:896 + NCH] = keepf[None, :]
    c[:, 896 + NCH:896 + 2 * NCH] = keepb[None, :]
    if colsT is not None:
        c[:, 896 + 2 * NCH:896 + 2 * NCH + 128] = colsT
    o = 896 + 2 * NCH + 128
    c[:, o:o + 128] = c[:, 128:256] - c[:, 128 + 64:128 + 65]
    c[:, o + 128:o + 256] = c[:, 384:512] - c[:, 384 + 64:384 + 65]
    return c


def common_inputs(w_in_a, ln_v_g_a, ln_v_b_a, w_s_a, b_s_a, w_out_a, w_in_b, w_g2_b, b_g_b, gn_g_b, w_out_b, ln_g, ln_b):
    f = lambda a: np.ascontiguousarray(np.asarray(a, dtype=np.float32))
    return {
        "w_in_a": f(w_in_a[0]),
        "wsT": np.ascontiguousarray(f(w_s_a[0]).transpose(2, 0, 1).reshape(128, 2048)),
        "b_s_a": f(b_s_a[0]).reshape(1, 2048),
        "w_out_a": f(w_out_a[0]),
        "w_in_b": f(w_in_b[0]),
        "w_g2_b": f(w_g2_b[0]),
        "b_g_b": f(b_g_b[0]).reshape(1, 2048),
        "w_out_b": f(w_out_b[0]),
        "ln_g": f(ln_g),
        "ln_b": f(ln_b),
    }


def param_cols(ln_v_g_a, ln_v_b_a, gn_g_b, ln_g, ln_b):
    f = lambda a, n: np.asarray(a, np.float32).reshape(n, 128).T
    return np.ascontiguousarray(np.concatenate([f(ln_v_g_a[0], 32), f(ln_v_b_a[0], 32), f(gn_g_b[0], 32), f(ln_g[0], 16), f(ln_b[0], 16)], 1))


_NC_CACHE = {}


def kernel(x_prompt, x_sample, w_in_a, ln_v_g_a, ln_v_b_a, w_s_a, b_s_a, w_out_a,
           w_in_b, w_g2_b, b_g_b, gn_g_b, w_out_b, ln_g, ln_b):
    x_prompt = np.asarray(x_prompt, np.float32)
    x_sample = np.asarray(x_sample, np.float32)
    X = np.concatenate([x_prompt.reshape(-1, D), x_sample.reshape(-1, D)], 0)
    NTOK = X.shape[0]
    starts = {0, x_prompt.shape[0] * x_prompt.shape[1], x_prompt.shape[0] * x_prompt.shape[1] + x_sample.shape[1]}
    NT = (OWN + 2 * HALO) // 512
    NCH = NT * 4
    common = common_inputs(w_in_a, ln_v_g_a, ln_v_b_a, w_s_a, b_s_a, w_out_a, w_in_b, w_g2_b, b_g_b, gn_g_b, w_out_b, ln_g, ln_b)
    colsT = param_cols(ln_v_g_a, ln_v_b_a, gn_g_b, ln_g, ln_b)
    in_maps = []
    for c in range(NCORES):
        lo = c * OWN - HALO
        xw = np.zeros((NT * 512, D), np.float32)
        a, b = max(lo, 0), min(lo + NT * 512, NTOK)
        xw[a - lo:b - lo] = X[a:b]
        keepf = np.ones(NCH, np.float32)
        keepb = np.ones(NCH, np.float32)
        for n in range(NCH):
            g0 = lo + n * 128
            if g0 in starts or g0 <= 0 or g0 >= NTOK:
                keepf[n] = 0.0
            g1 = g0 + 128
            if g1 in starts or g1 <= 0 or g1 >= NTOK:
                keepb[n] = 0.0
        m = dict(common)
        m["xw"] = xw
        m["cst"] = make_consts(NCH, keepf, keepb, colsT)
        in_maps.append(m)
    key = (NT,)
    if key not in _NC_CACHE:
        _NC_CACHE[key] = build(NT, 1, NT - 1)
    nc = _NC_CACHE[key]
    res = run_bass_kernel_spmd(nc, in_maps, core_ids=list(range(NCORES)))
    Y = np.concatenate([np.asarray(res.results[c]["y"], np.float32) for c in range(NCORES)], 0)
    n_p = x_prompt.shape[0] * x_prompt.shape[1]
    y_prompt = Y[:n_p].reshape(x_prompt.shape)
    y_sample = Y[n_p:].reshape(x_sample.shape)
    return (y_prompt, y_sample)
```

```python
import os
import numpy as np
import ml_dtypes
KSKIP = os.environ.get('KSKIP', '')
from contextlib import ExitStack
import concourse.bass as bass
import concourse.mybir as mybir
from concourse.bass_utils import run_bass_kernel_spmd

F32 = mybir.dt.float32
BF16 = mybir.dt.bfloat16
AF = mybir.ActivationFunctionType
ALU = mybir.AluOpType

D = 2048
DI = 4096
ALPHA = 4.0 ** 0.25
LN_EPS = 1e-5
RMS_EPS = 1e-6
NCORES = 8
OWN = 5120
HALO = 512


class Buf:
    __slots__ = ("name", "w", "r", "dsem", "dcnt")

    def __init__(self, name):
        self.name = name
        self.w = None
        self.r = {}
        self.dsem = None
        self.dcnt = 0


class Sched:
    CE = ("pe", "act", "dve", "pool")

    def __init__(self, nc, es):
        self.nc = nc
        self.es = es
        self.q = {e: [] for e in ("pe", "act", "dve", "pool", "sp")}
        self.sems = {e: es.enter_context(nc.semaphore("sem_" + e)) for e in self.CE}
        self.cnt = {e: 0 for e in self.CE}
        self.seen = {e: {} for e in self.q}
        self.dbufs = []

    def _semof(self, k):
        return self.sems[k] if isinstance(k, str) else k.dsem

    def _need(self, e, toks):
        best = {}
        for t in toks:
            if t is None:
                continue
            k, v = t
            if k == e and e == "pe":
                continue
            if self.seen[e].get(k, 0) >= v:
                continue
            if best.get(k, 0) < v:
                best[k] = v
        for k, v in best.items():
            self.seen[e][k] = v
            sem = self._semof(k)
            self.q[e].append(lambda E, sem=sem, v=v: E.wait_ge(sem, v))

    def _deps(self, reads, writes):
        toks = []
        for b in reads:
            toks.append(b.w)
        for b in writes:
            toks.append(b.w)
            toks.extend(b.r.items())
        return toks

    def _update(self, tok, reads, writes):
        for b in writes:
            b.w = tok
            b.r = {}
        for b in reads:
            if b not in writes:
                k, v = tok
                if b.r.get(k, 0) < v:
                    b.r[k] = v

    mute = False

    def op(self, e, fn, reads=(), writes=(), inc=True):
        if self.mute:
            return
        self._need(e, self._deps(reads, writes))
        if inc:
            self.cnt[e] += 1
            v = self.cnt[e]
            sem = self.sems[e]
            self.q[e].append(lambda E, fn=fn, sem=sem: fn(E).then_inc(sem, 1))
            tok = (e, v)
        else:
            self.q[e].append(lambda E, fn=fn: fn(E))
            tok = (e, self.cnt[e] + 1)
        self._update(tok, reads, writes)

    def dma(self, out_ap, in_ap, reads, writes, sb, q="sp"):
        if self.mute:
            return
        if sb.dsem is None:
            sb.dsem = self.es.enter_context(self.nc.semaphore("d_" + sb.name))
            self.dbufs.append(sb)
        toks = self._deps(reads, writes)
        if sb.dcnt > 0:
            toks.append((sb, sb.dcnt))
        self._need(q, toks)
        sb.dcnt += 16
        v = sb.dcnt
        sem = sb.dsem
        self.q[q].append(lambda E, o=out_ap, i=in_ap, sem=sem: E.dma_start(out=o, in_=i).then_inc(sem, 16))
        self._update((sb, v), reads, writes)

    def barrier(self):
        toks = [(e, self.cnt[e]) for e in self.CE if self.cnt[e] > 0]
        toks += [(b, b.dcnt) for b in self.dbufs if b.dcnt > 0]
        for e in self.q:
            self._need(e, [t for t in toks if t[0] != e or e != "pe"])

    def finish(self):
        with self.nc.Block() as block:
            q = self.q

            @block.sync
            def _(E):
                for t in q["sp"]:
                    t(E)

            @block.tensor
            def _(E):
                for t in q["pe"]:
                    t(E)

            @block.scalar
            def _(E):
                for t in q["act"]:
                    t(E)

            @block.vector
            def _(E):
                for t in q["dve"]:
                    t(E)

            @block.gpsimd
            def _(E):
                for t in q["pool"]:
                    t(E)


def build(NT, T0, T1, stage=9):
    NW = NT * 512
    NCH = NT * 4
    NOWN = (T1 - T0) * 512
    nc = bass.Bass("TRN2", target_bir_lowering=False)

    def din(name, shape, dt=F32):
        return nc.dram_tensor(name, list(shape), dt, kind="ExternalInput").ap()

    def dscr(name, shape, dt):
        return nc.dram_tensor(name, list(shape), dt, kind="Internal").ap()

    xw = din("xw", [NW, D])
    w_in_a = din("w_in_a", [D, 3 * DI])
    wsT_in = din("wsT", [128, 2048])
    b_s = din("b_s_a", [1, 16 * 128])
    w_out_a = din("w_out_a", [DI, D])
    w_in_b = din("w_in_b", [D, 10272])
    w_g2 = din("w_g2_b", [2, 16, 1024])
    b_g = din("b_g_b", [1, 2048])
    w_out_b = din("w_out_b", [DI, D])
    ln_g = din("ln_g", [2, D])
    ln_b = din("ln_b", [2, D])
    CW = 7 * 128 + 2 * NCH + 128 + 256
    cst = din("cst", [128, CW])
    yout = nc.dram_tensor("y", [NOWN, D], F32, kind="ExternalOutput").ap()

    wA_in_s = dscr("wA_in_s", [24, 128, 16 * 512], BF16)
    wA_out_s = dscr("wA_out_s", [8, 128, 16 * 512], BF16)
    wB_in_s = dscr("wB_in_s", [20, 128, 16 * 512], BF16)
    wB_out_s = dscr("wB_out_s", [8, 128, 16 * 512], BF16)
    x1h_s = dscr("x1h_s", [NW, D], F32)
    qT_s = dscr("qT_s", [NCH, 128, 8, 128], BF16)
    kT_s = dscr("kT_s", [NCH, 128, 8, 128], BF16)
    zT_s = dscr("zT_s", [NCH, 128, 32, 128], BF16)
    ktok_s = dscr("ktok_s", [NW, 1024], BF16)
    v_s = dscr("v_s", [NW, DI], BF16)
    sp_s = dscr("sp_s", [NW, 4096], BF16)
    of_s = dscr("of_s", [NW, DI], F32)
    yT_s = dscr("yT_s", [NCH, 128, 32, 128], BF16)

    es_all = ExitStack()
    with es_all:
        S = Sched(nc, es_all)
        bufs = {}

        def B(name):
            if name not in bufs:
                bufs[name] = Buf(name.replace("/", "_").replace(":", "_"))
            return bufs[name]

        psum = [es_all.enter_context(nc.psum_tensor(f"ps{i}", [128, 512], F32)) for i in range(8)]
        psb = [Buf(f"ps{i}") for i in range(8)]
        pstate = {"i": 0}

        def nps():
            i = pstate["i"]
            pstate["i"] = (i + 1) % 8
            return psum[i], psb[i]

        E = es_all.enter_context
        cst_t = E(nc.sbuf_tensor("cst_t", [128, CW], F32))
        tribf = E(nc.sbuf_tensor("tribf", [128, 6, 128], BF16))
        identb = E(nc.sbuf_tensor("identb", [128, 128], BF16))
        cols = cst_t[:, 896 + 2 * NCH:896 + 2 * NCH + 128]
        Bc = B("consts")
        ident = cst_t[:, 0:128]
        tri = [tribf[:, 0, :], tribf[:, 2, :]]
        triR = [tribf[:, 1, :], tribf[:, 3, :]]
        maskT = [cst_t[:, 640:768], cst_t[:, 768:896]]
        keep = [cst_t[:, 896:896 + NCH], cst_t[:, 896 + NCH:896 + 2 * NCH]]

        epsc = E(nc.sbuf_tensor("epsc", [128, 2], F32))
        S.op("dve", lambda e: e.memset(epsc[:, 0:1], LN_EPS), [], [Bc])
        S.op("dve", lambda e: e.memset(epsc[:, 1:2], RMS_EPS), [], [Bc])
        S.dma(cst_t[:], cst[:, :], [], [Bc], Bc)
        S.op("dve", lambda e: e.tensor_copy(out=identb[:], in_=ident), [Bc], [Bc])
        S.op("dve", lambda e: e.tensor_copy(out=tribf[:, 0:4, :].rearrange("p a b -> p (a b)"), in_=cst_t[:, 128:640]), [Bc], [Bc])
        S.op("dve", lambda e: e.tensor_copy(out=tribf[:, 4:6, :].rearrange("p a b -> p (a b)"), in_=cst_t[:, CW - 256:CW]), [Bc], [Bc])
        triC = [tribf[:, 4, :], tribf[:, 5, :]]

        lnvg = cols[:, 0:32]
        lnvb = cols[:, 32:64]
        gncol = cols[:, 64:96]
        l0g = cols[:, 96:112]
        l0b = cols[:, 112:128]

        with ExitStack() as es0:
            S.mute = ('b' in KSKIP)
            E0 = es0.enter_context
            st32 = [E0(nc.sbuf_tensor(f"st32_{i}", [128, 8, 512], F32)) for i in range(6)]
            st16 = [E0(nc.sbuf_tensor(f"st16_{i}", [128, 8 * 512], BF16)) for i in range(6)]
            jobs = []
            wa = w_in_a.rearrange("(k p) n -> p k n", p=128)
            for g in range(24):
                for h in range(2):
                    jobs.append((wa[:, h * 8:(h + 1) * 8, g * 512:(g + 1) * 512], wA_in_s[g, :, h * 4096:(h + 1) * 4096]))
            wb = w_in_b.rearrange("(k p) n -> p k n", p=128)
            for g in range(20):
                for h in range(2):
                    jobs.append((wb[:, h * 8:(h + 1) * 8, g * 512:(g + 1) * 512], wB_in_s[g, :, h * 4096:(h + 1) * 4096]))
            for (wsrc, wdst) in ((w_out_a, wA_out_s), (w_out_b, wB_out_s)):
                wo = wsrc.rearrange("(j p) n -> p j n", p=128)
                for n in range(4):
                    for hf in range(2):
                        for qq in range(2):
                            j0 = hf * 16 + qq * 8
                            jobs.append((wo[:, j0:j0 + 8, n * 512:(n + 1) * 512],
                                         wdst[n * 2 + hf, :, qq * 4096:(qq + 1) * 4096]))
            engs = ["act", "dve"]
            if stage < 1:
                jobs = []
            NSL = 6

            def p0_load(idx):
                if idx < len(jobs):
                    s_ = idx % NSL
                    S.dma(st32[s_][:], jobs[idx][0], [], [B(f"st32_{s_}")], B(f"st32_{s_}"))

            for idx in range(NSL):
                p0_load(idx)
            for idx, (src, dst) in enumerate(jobs):
                s = idx % NSL
                b32 = B(f"st32_{s}")
                b16 = B(f"st16_{s}")
                en = engs[idx % 2]
                src_flat = st32[s][:].rearrange("p k c -> p (k c)")
                if en == "act":
                    S.op("act", lambda e, s=s, sf=src_flat: e.activation(out=st16[s][:], in_=sf, func=AF.Copy), [b32], [b16])
                else:
                    S.op(en, lambda e, s=s, sf=src_flat: e.tensor_copy(out=st16[s][:], in_=sf), [b32], [b16])
                S.dma(dst, st16[s][:], [b16], [B("wscr")], b16)
                p0_load(idx + NSL)
            S.mute = False
            S.barrier()

        wgl = E(nc.sbuf_tensor("wgl", [128, 16, 32], BF16))
        W2h = E(nc.sbuf_tensor("W2h", [33, 2048], BF16))
        W2l = E(nc.sbuf_tensor("W2l", [33, 2048], BF16))
        glh = E(nc.sbuf_tensor("glh", [33, 512], BF16))
        gll = E(nc.sbuf_tensor("gll", [33, 512], BF16))
        with ExitStack() as es0:
            S.mute = ('c' in KSKIP)
            E0 = es0.enter_context
            wgl32 = E0(nc.sbuf_tensor("wgl32", [128, 16, 32], F32))
            W2 = E0(nc.sbuf_tensor("W2", [33, 2048], F32))
            Bw = B("wgl32")
            S.dma(wgl32[:], w_in_b.rearrange("(k p) n -> p k n", p=128)[:, :, 10240:10272], [], [Bw], Bw)
            S.op("dve", lambda e: e.tensor_copy(out=wgl[:], in_=wgl32[:]), [Bw], [Bc])
            S.op("dve", lambda e: e.memset(W2[:], 0.0), [], [Bc])
            S.op("dve", lambda e: e.memset(glh[:], 1.0), [], [B("glT")])
            S.op("dve", lambda e: e.memset(gll[:], 0.0), [], [B("glT")])
            S.dma(W2[0:16, 0:1024], w_g2[0, :, :], [], [Bc], Bc)
            S.dma(W2[16:32, 1024:2048], w_g2[1, :, :], [], [Bc], Bc)
            S.dma(W2[32:33, :], b_g[:, :], [], [Bc], Bc)
            S.op("dve", lambda e: e.tensor_copy(out=W2h[:], in_=W2[:]), [Bc], [B("W2h")])
            S.op("dve", lambda e: e.tensor_tensor(out=W2l[:], in0=W2[:], in1=W2h[:], op=ALU.subtract), [Bc, B("W2h")], [B("W2l")])
            S.mute = False
            S.barrier()

        with ExitStack() as es1:
            E1 = es1.enter_context
            WsTb = E1(nc.sbuf_tensor("WsTb", [128, 16, 128], BF16))
            Bias = E1(nc.sbuf_tensor("Bias", [128, 32, 128], F32))
            with ExitStack() as es0:
                S.mute = ('d' in KSKIP)
                E0 = es0.enter_context
                WsTf = E0(nc.sbuf_tensor("WsTf", [128, 16, 128], F32))
                WsTl = E0(nc.sbuf_tensor("WsTl", [128, 16, 128], BF16))
                onesb = E0(nc.sbuf_tensor("onesb", [128, 128], BF16))
                Rb = E0(nc.sbuf_tensor("Rb", [128, 16, 128], F32))
                bsb = E0(nc.sbuf_tensor("bsb", [128, 16 * 128], F32))
                Bs = B("spsetup")
                S.dma(WsTf[:].rearrange("p a b -> p (a b)"), wsT_in[:, :], [], [Bs], Bs)
                S.dma(bsb[:], b_s[0:1, :].broadcast_to([128, 2048]), [], [B("bsb")], B("bsb"))
                S.op("dve", lambda e: e.memset(onesb[:], 1.0), [], [B("onesb")])
                S.op("dve", lambda e: e.tensor_copy(out=WsTb[:], in_=WsTf[:]), [Bs], [B("WsTb")])
                S.op("dve", lambda e: e.tensor_tensor(out=WsTl[:], in0=WsTf[:], in1=WsTb[:], op=ALU.subtract), [Bs, B("WsTb")], [B("WsTl")])
                for g4 in range(4):
                    pt, pb_ = nps()
                    for gg in range(4):
                        g = g4 * 4 + gg
                        S.op("pe", lambda e, g=g, gg=gg, pt=pt: e.matmul(pt[:, gg * 128:(gg + 1) * 128], lhsT=onesb[:], rhs=WsTb[:, g, :],
                                                                  start=True, stop=False), [B("onesb"), B("WsTb"), B("WsTl")], [pb_], inc=False)
                        S.op("pe", lambda e, g=g, gg=gg, pt=pt: e.matmul(pt[:, gg * 128:(gg + 1) * 128], lhsT=onesb[:], rhs=WsTl[:, g, :],
                                                                  start=False, stop=True), [B("onesb"), B("WsTb"), B("WsTl")], [pb_], inc=(gg == 3))
                    S.op("dve", lambda e, g4=g4, pt=pt: e.tensor_copy(out=Rb[:, g4 * 4:(g4 + 1) * 4, :].rearrange("p a b -> p (a b)"), in_=pt[:]),
                         [pb_], [B("Rb")])
                for j in range(32 if 'w' not in KSKIP else 0):
                    g = j // 2
                    S.op("dve", lambda e, j=j, g=g: e.scalar_tensor_tensor(out=Bias[:, j, :], in0=Rb[:, g, :], scalar=lnvb[:, j:j + 1],
                                                                         in1=bsb[:, g * 128:(g + 1) * 128], op0=ALU.mult, op1=ALU.add),
                         [B("Rb"), B("bsb"), Bc], [B("Bias")])
                S.mute = False
                S.barrier()

            bigA = E1(nc.sbuf_tensor("bigA", [128, 4, DI], BF16))
            xT = E1(nc.sbuf_tensor("xT", [128, 16, 512], BF16))
            pbuf = E1(nc.sbuf_tensor("pbuf", [128, 32, 512], BF16))
            wsl = [E1(nc.sbuf_tensor(f"wsl{i}", [128, 16, 512], BF16)) for i in range(2)]
            xin = [E1(nc.sbuf_tensor(f"xin{i}", [128, D], F32)) for i in range(2)]
            xb = [E1(nc.sbuf_tensor(f"xb{i}", [128, D], BF16)) for i in range(2)]
            tmpb = [E1(nc.sbuf_tensor(f"tmpb{i}", [128, 512], BF16)) for i in range(2)]
            tmpf = [E1(nc.sbuf_tensor(f"tmpf{i}", [128, 512], F32)) for i in range(2)]
            stg = [E1(nc.sbuf_tensor(f"stg{i}", [128, 4, 512], BF16)) for i in range(2)]
            stf = [E1(nc.sbuf_tensor(f"stf{i}", [128, 512], F32)) for i in range(2)]
            sps = [E1(nc.sbuf_tensor(f"sps{i}", [128, 2, 512], BF16)) for i in range(2)]
            stats = E1(nc.sbuf_tensor("stats", [128, 4, 8, 6], F32))
            mv = E1(nc.sbuf_tensor("mv", [128, 4, 2], F32))
            rstd = E1(nc.sbuf_tensor("rstd", [128, 4], F32))

            wlist = []
            for g in range(8):
                wlist.append(wA_in_s[g])
                wlist.append(wA_in_s[16 + g])
            for n in range(8):
                wlist.append(wA_in_s[8 + n])
            for n in range(8):
                wlist.append(wA_out_s[n])
            wseq = []
            for ti in range(NT):
                wseq.extend(wlist)
                is_own = T0 <= ti < T1
                for g in list(range(0, 4)) + list(range(12, 20)):
                    if is_own or g in (2, 3):
                        wseq.append(wB_in_s[g])
                for g in range(2, 12):
                    wseq.append(wB_in_s[g])
            wstate = {"issued": 0, "used": 0}
            total_w = len(wseq)

            def w_issue():
                k = wstate["issued"]
                if k >= total_w:
                    return
                s = k % 2
                bw = B(f"wsl{s}")
                S.dma(wsl[s][:].rearrange("p k c -> p (k c)"), wseq[k], [B("wscr")], [bw], bw)
                wstate["issued"] = k + 1

            def w_next():
                k = wstate["used"]
                while wstate["issued"] <= k:
                    w_issue()
                wstate["used"] = k + 1
                return wsl[k % 2], B(f"wsl{k % 2}")

            def w_done():
                w_issue()

            if stage >= 2:
                w_issue()
                w_issue()
            rr = {"i": 0}

            def alt(lst):
                rr["i"] += 1
                return rr["i"] % len(lst)

            def transpose_block(src_bf, srcB, b, affine):
                for half in range(2):
                    pt, pb_ = nps()
                    ptb = pt[:].bitcast(BF16)
                    for kk in range(8):
                        k = half * 8 + kk
                        S.op("pe", lambda e, k=k, kk=kk, ptb=ptb: e.transpose(out=ptb[:, kk * 128:(kk + 1) * 128], in_=src_bf[:, k * 128:(k + 1) * 128],
                                                                               identity=identb[:]), [srcB, Bc], [pb_], inc=(kk == 7))
                    if not affine:
                        S.op("act", lambda e, half=half, ptb=ptb: e.activation(out=xT[:, half * 8:(half + 1) * 8, b * 128:(b + 1) * 128],
                                                                             in_=ptb.rearrange("p (a c) -> p a c", a=8), func=AF.Copy),
                             [pb_], [B(f"xT{b}")])
                    else:
                        for kk in range(8):
                            k = half * 8 + kk
                            S.op("dve", lambda e, k=k, kk=kk, ptb=ptb: e.tensor_scalar(out=xT[:, k, b * 128:(b + 1) * 128], in0=ptb[:, kk * 128:(kk + 1) * 128],
                                                                                scalar1=l0g[:, k:k + 1], scalar2=l0b[:, k:k + 1], op0=ALU.mult, op1=ALU.add),
                                 [pb_, Bc], [B(f"xT{b}")])

            xTB = [B(f"xT{b}") for b in range(4)]
            bigB = [B(f"bigA{b}") for b in range(4)]

            def x_load(ti, b):
                s_ = b % 2
                bx = B(f"xin{s_}")
                S.dma(xin[s_][:], xw[(ti * 4 + b) * 128:(ti * 4 + b + 1) * 128, :], [], [bx], bx)

            def x_cast(b):
                s_ = b % 2
                if b % 2 == 0:
                    S.op("act", lambda e, s_=s_: e.activation(out=xb[s_][:], in_=xin[s_][:], func=AF.Copy), [B(f"xin{s_}")], [B(f"xb{s_}")])
                else:
                    S.op("dve", lambda e, s_=s_: e.tensor_copy(out=xb[s_][:], in_=xin[s_][:]), [B(f"xin{s_}")], [B(f"xb{s_}")])

            for i in range(NT if stage >= 2 else 0):
                if i == 0:
                    x_load(0, 0)
                    x_load(0, 1)
                    x_cast(0)
                    x_cast(1)
                    x_load(0, 2)
                    x_load(0, 3)
                transpose_block(xb[0], B("xb0"), 0, False)
                transpose_block(xb[1], B("xb1"), 1, False)
                x_cast(2)
                x_cast(3)
                transpose_block(xb[0], B("xb0"), 2, False)
                transpose_block(xb[1], B("xb1"), 3, False)
                if i + 1 < NT:
                    x_load(i + 1, 0)
                    x_load(i + 1, 1)
                for g in range(8):
                    for which in range(2):
                        W, WB = w_next()
                        for jj in range(4):
                            j = g * 4 + jj
                            pt, pb_ = nps()
                            for k in range(16):
                                S.op("pe", lambda e, k=k, jj=jj, W=W, pt=pt: e.matmul(pt[:], lhsT=W[:, k, jj * 128:(jj + 1) * 128], rhs=xT[:, k, :],
                                                                                  start=(k == 0), stop=(k == 15)), [WB] + xTB, [pb_], inc=(k == 15))
                            if which == 0:
                                S.op("act", lambda e, j=j, pt=pt: e.activation(out=pbuf[:, j, :], in_=pt[:], func=AF.Gelu_apprx_tanh), [pb_], [B(f"p{j}")])
                            else:
                                t = alt(tmpb)
                                S.op("act", lambda e, t=t, pt=pt: e.activation(out=tmpb[t][:], in_=pt[:], func=AF.Silu), [pb_], [B(f"tmpb{t}")])
                                S.op("pool", lambda e, t=t, j=j: e.tensor_tensor(out=pbuf[:, j, :], in0=pbuf[:, j, :], in1=tmpb[t][:], op=ALU.mult),
                                     [B(f"tmpb{t}")], [B(f"p{j}")])
                        w_done()
                for n in range(8):
                    W, WB = w_next()
                    for b in range(4):
                        pt, pb_ = nps()
                        for k in range(16):
                            S.op("pe", lambda e, k=k, b=b, W=W, pt=pt: e.matmul(pt[:], lhsT=xT[:, k, b * 128:(b + 1) * 128], rhs=W[:, k, :],
                                                                          start=(k == 0), stop=(k == 15)), [WB, xTB[b]], [pb_], inc=(k == 15))
                        S.op("act", lambda e, b=b, n=n, pt=pt: e.activation(out=bigA[:, b, n * 512:(n + 1) * 512], in_=pt[:], func=AF.Gelu_apprx_tanh),
                             [pb_], [bigB[b]])
                        S.op("dve", lambda e, b=b, n=n: e.bn_stats(out=stats[:, b, n, :], in_=bigA[:, b, n * 512:(n + 1) * 512]), [bigB[b]], [B(f"stats{b}")])
                    w_done()
                for b in range(4):
                    S.op("dve", lambda e, b=b: e.bn_aggr(out=mv[:, b, :], in_=stats[:, b, :, :].rearrange("p a c -> p (a c)")), [B(f"stats{b}")], [B(f"mv{b}")])
                    S.op("act", lambda e, b=b: e.activation(out=rstd[:, b:b + 1], in_=mv[:, b, 1:2], func=AF.Ln, bias=epsc[:, 0:1]), [B(f"mv{b}"), Bc], [B(f"rstd{b}")])
                    S.op("act", lambda e, b=b: e.activation(out=rstd[:, b:b + 1], in_=rstd[:, b:b + 1], func=AF.Exp, scale=-0.5), [], [B(f"rstd{b}")])
                    S.op("dve", lambda e, b=b: e.tensor_scalar(out=bigA[:, b, :], in0=bigA[:, b, :], scalar1=mv[:, b, 0:1], scalar2=rstd[:, b:b + 1],
                                                               op0=ALU.subtract, op1=ALU.mult), [B(f"mv{b}"), B(f"rstd{b}")], [bigB[b]])
                for j in range(32):
                    g = j // 2
                    pt, pb_ = nps()
                    for b in range(4):
                        S.op("pe", lambda e, b=b, j=j, g=g, pt=pt: e.matmul(pt[:, b * 128:(b + 1) * 128], lhsT=bigA[:, b, j * 128:(j + 1) * 128], rhs=WsTb[:, g, :],
                                                                      start=True, stop=True), [bigB[b], B("WsTb")], [pb_], inc=(b == 3))
                    t = alt(tmpf)
                    S.op("dve", lambda e, t=t, j=j, pt=pt: e.scalar_tensor_tensor(out=tmpf[t][:].rearrange("p (a c) -> p a c", a=4),
                                                                             in0=pt[:].rearrange("p (a c) -> p a c", a=4), scalar=lnvg[:, j:j + 1],
                                                                             in1=Bias[:, j:j + 1, :].broadcast_to([128, 4, 128]), op0=ALU.mult, op1=ALU.add),
                         [pb_, B("Bias"), Bc], [B(f"tmpf{t}")])
                    S.op("pool", lambda e, t=t, j=j: e.tensor_tensor(out=pbuf[:, j, :], in0=pbuf[:, j, :], in1=tmpf[t][:], op=ALU.mult),
                         [B(f"tmpf{t}")], [B(f"p{j}")])
                rA = [bigA[:, b, :].bitcast(F32) for b in range(4)]
                for b in range(4):
                    S.dma(rA[b], xw[(i * 4 + b) * 128:(i * 4 + b + 1) * 128, :], [], [bigB[b]], bigB[b])
                pB = [B(f"p{j}") for j in range(32)]
                for n in range(4):
                    pts = [nps() for _ in range(4)]
                    for hf in range(2):
                        W, WB = w_next()
                        for b in range(4):
                            pt, pb_ = pts[b]
                            for jj in range(16):
                                j = hf * 16 + jj
                                S.op("pe", lambda e, b=b, j=j, jj=jj, W=W, pt=pt, hf=hf: e.matmul(pt[:], lhsT=pbuf[:, j, b * 128:(b + 1) * 128], rhs=W[:, jj, :],
                                                                                          start=(hf == 0 and jj == 0), stop=(hf == 1 and jj == 15)),
                                     [WB] + pB, [pb_], inc=(jj == 15))
                        w_done()
                    for b in range(4):
                        pt, pb_ = pts[b]
                        S.op("dve", lambda e, b=b, n=n, pt=pt: e.scalar_tensor_tensor(out=rA[b][:, n * 512:(n + 1) * 512], in0=rA[b][:, n * 512:(n + 1) * 512],
                                                                                 scalar=ALPHA, in1=pt[:], op0=ALU.mult, op1=ALU.add), [pb_], [bigB[b]])
                        S.op("dve", lambda e, b=b, n=n: e.bn_stats(out=stats[:, b, n, :], in_=rA[b][:, n * 512:(n + 1) * 512]), [bigB[b]], [B(f"stats{b}")])
                for b in range(4):
                    S.op("dve", lambda e, b=b: e.bn_aggr(out=mv[:, b, :], in_=stats[:, b, 0:4, :].rearrange("p a c -> p (a c)")), [B(f"stats{b}")], [B(f"mv{b}")])
                    S.op("act", lambda e, b=b: e.activation(out=rstd[:, b:b + 1], in_=mv[:, b, 1:2], func=AF.Ln, bias=epsc[:, 0:1]), [B(f"mv{b}"), Bc], [B(f"rstd{b}")])
                    S.op("act", lambda e, b=b: e.activation(out=rstd[:, b:b + 1], in_=rstd[:, b:b + 1], func=AF.Exp, scale=-0.5), [], [B(f"rstd{b}")])
                    S.op("dve", lambda e, b=b: e.tensor_scalar(out=rA[b], in0=rA[b], scalar1=mv[:, b, 0:1], scalar2=rstd[:, b:b + 1],
                                                               op0=ALU.subtract, op1=ALU.mult), [B(f"mv{b}"), B(f"rstd{b}")], [bigB[b]])
                    S.dma(x1h_s[(i * 4 + b) * 128:(i * 4 + b + 1) * 128, :], rA[b], [bigB[b]], [B(f"x1h{i * 4 + b}")], bigB[b])
                    s = b % 2
                    S.op("act", lambda e, b=b, s=s: e.activation(out=xb[s][:], in_=rA[b], func=AF.Copy), [bigB[b]], [B(f"xb{s}")])
                    transpose_block(xb[s], B(f"xb{s}"), b, True)
                own_tile = T0 <= i < T1
                for gi, g in enumerate(list(range(0, 4)) + list(range(12, 20))):
                    if not own_tile and g not in (2, 3):
                        continue
                    W, WB = w_next()
                    sg = alt(stg)
                    bsg = B(f"stg{sg}")
                    for jj in range(4):
                        pt, pb_ = nps()
                        for k in range(16):
                            S.op("pe", lambda e, k=k, jj=jj, W=W, pt=pt: e.matmul(pt[:], lhsT=W[:, k, jj * 128:(jj + 1) * 128], rhs=xT[:, k, :],
                                                                              start=(k == 0), stop=(k == 15)), [WB] + xTB, [pb_], inc=(k == 15))
                        if g < 2:
                            S.op("act", lambda e, jj=jj, sg=sg, pt=pt: e.activation(out=stg[sg][:, jj, :], in_=pt[:], func=AF.Copy, scale=0.0625), [pb_], [bsg])
                        elif g < 4:
                            S.op("act", lambda e, jj=jj, sg=sg, pt=pt: e.activation(out=stg[sg][:, jj, :], in_=pt[:], func=AF.Copy), [pb_], [bsg])
                        else:
                            S.op("act", lambda e, jj=jj, sg=sg, pt=pt: e.activation(out=stg[sg][:, jj, :], in_=pt[:], func=AF.Silu), [pb_], [bsg])
                            jz = (g - 12) * 4 + jj
                            S.op("dve", lambda e, jj=jj, sg=sg, jz=jz: e.tensor_scalar(out=stg[sg][:, jj, :], in0=stg[sg][:, jj, :], scalar1=gncol[:, jz:jz + 1],
                                                                                     scalar2=None, op0=ALU.mult), [Bc], [bsg])
                    w_done()
                    for b in range(4):
                        ch = i * 4 + b
                        if g < 2:
                            dst = qT_s[ch, :, g * 4:(g + 1) * 4, :]
                        elif g < 4:
                            dst = kT_s[ch, :, (g - 2) * 4:(g - 1) * 4, :]
                        else:
                            dst = zT_s[ch, :, (g - 12) * 4:(g - 11) * 4, :]
                        S.dma(dst, stg[sg][:, :, b * 128:(b + 1) * 128], [bsg], [B(f"fm{ch}")], bsg)
                for g in range(2, 12):
                    W, WB = w_next()
                    sg = alt(stg)
                    bsg = B(f"stg{sg}")
                    for b in range(4):
                        pt, pb_ = nps()
                        for k in range(16):
                            S.op("pe", lambda e, k=k, b=b, W=W, pt=pt: e.matmul(pt[:], lhsT=xT[:, k, b * 128:(b + 1) * 128], rhs=W[:, k, :],
                                                                          start=(k == 0), stop=(k == 15)), [WB, xTB[b]], [pb_], inc=(k == 15))
                        S.op("act", lambda e, b=b, sg=sg, pt=pt: e.activation(out=stg[sg][:, b, :], in_=pt[:], func=AF.Copy), [pb_], [bsg])
                    w_done()
                    if g < 4:
                        dst = ktok_s[i * 512:(i + 1) * 512, (g - 2) * 512:(g - 1) * 512]
                    else:
                        dst = v_s[i * 512:(i + 1) * 512, (g - 4) * 512:(g - 3) * 512]
                    S.dma(dst.rearrange("(b p) c -> p b c", p=128), stg[sg][:], [bsg], [B(f"tm{i}")], bsg, q="act")
                pt, pb_ = nps()
                for k in range(16):
                    S.op("pe", lambda e, k=k, pt=pt: e.matmul(pt[0:32, :], lhsT=wgl[:, k, :], rhs=xT[:, k, :], start=(k == 0), stop=(k == 15)),
                         [Bc] + xTB, [pb_], inc=(k == 15))
                S.op("dve", lambda e, pt=pt: e.tensor_copy(out=glh[0:32, :], in_=pt[0:32, :]), [pb_], [B("glT")])
                S.op("dve", lambda e, pt=pt: e.tensor_tensor(out=gll[0:32, :], in0=pt[0:32, :], in1=glh[0:32, :], op=ALU.subtract), [pb_], [B("glT")])
                for b in range(4):
                    for c4 in range(4):
                        pt, pb_ = nps()
                        bs_ = slice(b * 128, (b + 1) * 128)
                        cs_ = slice(c4 * 512, (c4 + 1) * 512)
                        S.op("pe", lambda e, bs_=bs_, cs_=cs_, pt=pt: e.matmul(pt[:], lhsT=glh[0:33, bs_], rhs=W2h[0:33, cs_], start=True, stop=False),
                             [B("glT"), B("W2h"), B("W2l")], [pb_], inc=False)
                        S.op("pe", lambda e, bs_=bs_, cs_=cs_, pt=pt: e.matmul(pt[:], lhsT=gll[0:33, bs_], rhs=W2h[0:33, cs_], start=False, stop=False),
                             [B("glT"), B("W2h"), B("W2l")], [pb_], inc=False)
                        S.op("pe", lambda e, bs_=bs_, cs_=cs_, pt=pt: e.matmul(pt[:], lhsT=glh[0:33, bs_], rhs=W2l[0:33, cs_], start=False, stop=True),
                             [B("glT"), B("W2h"), B("W2l")], [pb_])
                        t = alt(tmpf)
                        sf = alt(stf)
                        sq = alt(sps)
                        S.op("act", lambda e, t=t, pt=pt: e.activation(out=tmpf[t][:], in_=pt[:], func=AF.Exp, scale=-1.0), [pb_], [B(f"tmpf{t}")])
                        S.op("act", lambda e, t=t, sf=sf: e.activation(out=stf[sf][:], in_=tmpf[t][:], func=AF.Ln, bias=1.0), [B(f"tmpf{t}")], [B(f"stf{sf}")])
                        S.op("pool", lambda e, sf=sf, sq=sq: e.tensor_copy(out=sps[sq][:, 0, :], in_=stf[sf][:]), [B(f"stf{sf}")], [B(f"sps{sq}")])
                        S.op("dve", lambda e, sf=sf, sq=sq: e.tensor_tensor(out=sps[sq][:, 1, :], in0=stf[sf][:], in1=sps[sq][:, 0, :], op=ALU.subtract),
                             [B(f"stf{sf}")], [B(f"sps{sq}")])
                        d_ = c4 // 2
                        r_ = slice((i * 4 + b) * 128, (i * 4 + b + 1) * 128)
                        dst = sp_s[r_, d_ * 2048:(d_ + 1) * 2048].rearrange("p (a c) -> p a c", a=2)[:, :, (c4 % 2) * 512:(c4 % 2 + 1) * 512]
                        S.dma(dst, sps[sq][:], [B(f"sps{sq}")], [B(f"sp{i * 4 + b}")], B(f"sps{sq}"))
                if i + 1 < NT:
                    x_cast(0)
                    x_cast(1)
                    x_load(i + 1, 2)
                    x_load(i + 1, 3)
            S.barrier()

        prep_banks = [0, 1, 2, 3]
        head_banks = [4, 5, 6, 7]
        pst2 = {"p": 0, "h": 0}

        def nps_p():
            i = prep_banks[pst2["p"] % 4]
            pst2["p"] += 1
            return psum[i], psb[i]

        def nps_h():
            i = head_banks[pst2["h"] % 4]
            pst2["h"] += 1
            return psum[i], psb[i]

        with ExitStack() as es2:
            E2 = es2.enter_context
            Sst = E2(nc.sbuf_tensor("Sst", [128, 8, 1024], F32))
            Sbf = E2(nc.sbuf_tensor("Sbf", [128, 8, 1024], BF16))
            sp_t = E2(nc.sbuf_tensor("sp_t", [128, 2048], BF16))
            qT_t = E2(nc.sbuf_tensor("qT_t", [128, 8, 128], BF16))
            kT_t = E2(nc.sbuf_tensor("kT_t", [128, 8, 128], BF16))
            ktok_t = E2(nc.sbuf_tensor("ktok_t", [128, 1024], BF16))
            E4 = E2(nc.sbuf_tensor("E4", [128, 1024], F32))
            Ea = E2(nc.sbuf_tensor("Ea", [128, 3, 8, 128], F32))
            Kt = E2(nc.sbuf_tensor("Kt", [128, 1024], BF16))
            v_t = [E2(nc.sbuf_tensor(f"v_t{i}", [128, DI], BF16)) for i in range(2)]
            Kd = [E2(nc.sbuf_tensor(f"Kd{i}", [128, 1024], BF16)) for i in range(2)]
            Pa = [E2(nc.sbuf_tensor(f"Pa{i}", [128, 3, 8, 128], BF16)) for i in range(2)]
            scp = [E2(nc.sbuf_tensor(f"scp{i}", [128, 24], F32)) for i in range(2)]
            zT_t = E2(nc.sbuf_tensor("zT_t", [128, 32, 128], BF16))
            of_t = [E2(nc.sbuf_tensor(f"of_t{i}", [128, 1024], F32)) for i in range(2)]
            osb = [E2(nc.sbuf_tensor(f"osb{i}", [128, 1024], F32)) for i in range(2)]
            junk = E2(nc.sbuf_tensor("junk", [128, 1024], BF16))
            aTs = [E2(nc.sbuf_tensor(f"aTs{i}", [128, 4, 128], BF16)) for i in range(2)]
            sc = E2(nc.sbuf_tensor("sc", [128, 8], F32))
            onb = E2(nc.sbuf_tensor("onb", [128, DI], BF16))
            yT = [E2(nc.sbuf_tensor(f"yT{i}", [128, 32, 128], BF16)) for i in range(2)]

            def prep(n, d, do_out, p):
                nn = n + 1 if d == 0 else n - 1
                has_next = 0 <= nn < NCH
                last_col = 127 if d == 0 else 0
                Bsp, Bq, Bk, Bkt, Bv = B("sp_t"), B("qT_t"), B("kT_t"), B("ktok_t"), B(f"v_t{p}")
                r0 = n * 128
                S.dma(sp_t[:], sp_s[r0:r0 + 128, d * 2048:(d + 1) * 2048], [B(f"sp{n}")], [Bsp], Bsp)
                S.dma(ktok_t[:], ktok_s[r0:r0 + 128, :], [B(f"tm{n // 4}")], [Bkt], Bkt)
                S.dma(v_t[p][:], v_s[r0:r0 + 128, :], [B(f"tm{n // 4}")], [Bv], Bv)
                if do_out:
                    S.dma(qT_t[:], qT_s[n], [B(f"fm{n}")], [Bq], Bq)
                    S.dma(kT_t[:], kT_s[n], [B(f"fm{n}")], [Bk], Bk)
                gts = []
                for half in range(2):
                    pt, pb_ = nps_p()
                    for c4 in range(4):
                        dc = half * 4 + c4
                        S.op("pe", lambda e, dc=dc, c4=c4, pt=pt: e.matmul(pt[:, c4 * 128:(c4 + 1) * 128], lhsT=sp_t[:, dc * 128:(dc + 1) * 128], rhs=tri[d],
                                                                      start=True, stop=False), [Bsp, Bc], [pb_], inc=False)
                        S.op("pe", lambda e, dc=dc, c4=c4, pt=pt: e.matmul(pt[:, c4 * 128:(c4 + 1) * 128], lhsT=sp_t[:, 1024 + dc * 128:1024 + (dc + 1) * 128], rhs=tri[d],
                                                                      start=False, stop=True), [Bsp, Bc], [pb_], inc=(c4 == 3))
                    gts.append((pt, pb_))
                grs = []
                for half in range(2):
                    pt, pb_ = nps_p()
                    S.op("pe", lambda e, half=half, pt=pt: e.matmul(pt[:], lhsT=triR[d], rhs=sp_t[:, half * 512:(half + 1) * 512], start=True, stop=False),
                         [Bsp, Bc], [pb_], inc=False)
                    S.op("pe", lambda e, half=half, pt=pt: e.matmul(pt[:], lhsT=triR[d], rhs=sp_t[:, 1024 + half * 512:1024 + (half + 1) * 512], start=False, stop=True),
                         [Bsp, Bc], [pb_])
                    grs.append((pt, pb_))
                Bsc = B(f"scp{p}")
                scq = scp[p]
                for half in range(2):
                    pt, pb_ = gts[half]
                    pv = pt[:].rearrange("p (a c) -> p a c", a=4)
                    S.op("act", lambda e, half=half, pv=pv: e.activation(out=scq[:, 16 + half * 4:20 + half * 4], in_=pv[:, :, last_col], func=AF.Exp), [pb_], [Bsc])
                BE4, BKd, BKt = B("E4"), B(f"Kd{p}"), B("Kt")
                if has_next:
                    kcol = keep[d][:, nn:nn + 1]
                    S.op("dve", lambda e, kcol=kcol: e.tensor_scalar(out=scq[:, 16:24], in0=scq[:, 16:24], scalar1=kcol, scalar2=None, op0=ALU.mult), [Bc], [Bsc])
                    for half in range(2):
                        pt, pb_ = grs[half]
                        S.op("act", lambda e, half=half, pt=pt: e.activation(out=E4[:, half * 512:(half + 1) * 512], in_=pt[:], func=AF.Exp), [pb_], [BE4])
                    S.op("dve", lambda e, kcol=kcol: e.scalar_tensor_tensor(out=Kd[p][:], in0=ktok_t[:], scalar=kcol, in1=E4[:], op0=ALU.mult, op1=ALU.mult),
                         [Bkt, BE4, Bc], [BKd])
                BEa = B("Ea")
                if do_out:
                    for half in range(2):
                        pt, pb_ = gts[half]
                        S.op("act", lambda e, half=half, pt=pt: e.activation(out=Ea[:, 2, half * 4:(half + 1) * 4, :].rearrange("p a c -> p (a c)"), in_=pt[:], func=AF.Exp),
                             [pb_], [BEa])
                    gcs = []
                    for half in range(2):
                        pt, pb_ = nps_p()
                        for c4 in range(4):
                            dc = half * 4 + c4
                            S.op("pe", lambda e, dc=dc, c4=c4, pt=pt: e.matmul(pt[:, c4 * 128:(c4 + 1) * 128], lhsT=sp_t[:, dc * 128:(dc + 1) * 128], rhs=triC[d],
                                                                          start=True, stop=False), [Bsp, Bc], [pb_], inc=False)
                            S.op("pe", lambda e, dc=dc, c4=c4, pt=pt: e.matmul(pt[:, c4 * 128:(c4 + 1) * 128], lhsT=sp_t[:, 1024 + dc * 128:1024 + (dc + 1) * 128], rhs=triC[d],
                                                                          start=False, stop=True), [Bsp, Bc], [pb_], inc=(c4 == 3))
                        gcs.append((pt, pb_))
                    for half in range(2):
                        pc, pcb = gcs[half]
                        S.op("act", lambda e, half=half, pc=pc: e.activation(out=Ea[:, 0, half * 4:(half + 1) * 4, :].rearrange("p a c -> p (a c)"), in_=pc[:], func=AF.Exp),
                             [pcb], [BEa])
                        S.op("act", lambda e, half=half, pc=pc: e.activation(out=Ea[:, 1, half * 4:(half + 1) * 4, :].rearrange("p a c -> p (a c)"), in_=pc[:], func=AF.Exp,
                                                                          scale=-1.0), [pcb], [BEa])
                    BP0, BP1, BP2 = B(f"Pa{p}_0"), B(f"Pa{p}_1"), B(f"Pa{p}_2")
                    S.op("dve", lambda e: e.tensor_tensor(out=Pa[p][:, 0, :, :], in0=qT_t[:], in1=Ea[:, 0, :, :], op=ALU.mult), [Bq, BEa], [BP0])
                    S.op("dve", lambda e: e.tensor_tensor(out=Pa[p][:, 1, :, :], in0=kT_t[:], in1=Ea[:, 1, :, :], op=ALU.mult), [Bk, BEa], [BP1])
                    S.op("dve", lambda e: e.tensor_tensor(out=Pa[p][:, 2, :, :], in0=qT_t[:], in1=Ea[:, 2, :, :], op=ALU.mult), [Bq, BEa], [BP2])
                    for h in range(4):
                        pa, pab = nps_p()
                        for c2 in range(2):
                            dc = 2 * h + c2
                            S.op("pe", lambda e, dc=dc, c2=c2, pa=pa: e.matmul(pa[:, 0:128], lhsT=Pa[p][:, 1, dc, :], rhs=Pa[p][:, 0, dc, :], start=(c2 == 0), stop=(c2 == 1)),
                                 [BP0, BP1], [pab], inc=(c2 == 1))
                        S.op("dve", lambda e, h=h, pa=pa: e.tensor_tensor(out=aTs[p][:, h, :], in0=pa[:, 0:128], in1=maskT[d], op=ALU.mult), [pab, Bc], [B(f"aTs{p}_{h}")])

            def heads(n, d, do_out, final, p):
                nn = n + 1 if d == 0 else n - 1
                has_next = 0 <= nn < NCH
                r0 = n * 128
                Bv, BKd, Bsc = B(f"v_t{p}"), B(f"Kd{p}"), B(f"scp{p}")
                vt, Pq, Kq, scq = v_t[p], Pa[p], Kd[p], scp[p]
                for h in range(4):
                    if do_out:
                        Ba = B(f"aTs{p}_{h}")
                        aT_ap = aTs[p][:, h, :]
                        oi = alt(osb)
                        Bo = B(f"osb{oi}")
                        if final:
                            Bof = B(f"of_t{oi}")
                            S.dma(of_t[oi][:], of_s[r0:r0 + 128, h * 1024:(h + 1) * 1024], [B(f"of{n}")], [Bof], Bof)
                        for half in range(2):
                            po, pob = nps_h()
                            c0 = h * 1024 + half * 512
                            S.op("pe", lambda e, aT_ap=aT_ap, c0=c0, po=po: e.matmul(po[:], lhsT=aT_ap, rhs=vt[:, c0:c0 + 512], start=True, stop=False),
                                 [Ba, Bv], [pob], inc=False)
                            for c2 in range(2):
                                dc = 2 * h + c2
                                S.op("pe", lambda e, dc=dc, c2=c2, half=half, po=po: e.matmul(po[:], lhsT=Pq[:, 2, dc, :], rhs=Sbf[:, dc, half * 512:(half + 1) * 512],
                                                                                       start=False, stop=(c2 == 1)), [B(f"Pa{p}_2"), B(f"Sbf{dc}")], [pob], inc=(c2 == 1))
                            if not final:
                                S.op("act", lambda e, oi=oi, half=half, po=po: e.activation(out=osb[oi][:, half * 512:(half + 1) * 512], in_=po[:], func=AF.Copy), [pob], [Bo])
                            else:
                                S.op("dve", lambda e, oi=oi, half=half, po=po: e.tensor_tensor(out=osb[oi][:, half * 512:(half + 1) * 512], in0=po[:],
                                                                                           in1=of_t[oi][:, half * 512:(half + 1) * 512], op=ALU.add), [pob, Bof], [Bo])
                        if not final:
                            S.dma(of_s[r0:r0 + 128, h * 1024:(h + 1) * 1024], osb[oi][:], [Bo], [B(f"of{n}")], Bo, q="act")
                        else:
                            S.op("act", lambda e, oi=oi, h=h: e.activation(out=junk[:], in_=osb[oi][:], func=AF.Square, accum_out=sc[:, h:h + 1]), [Bo], [B("junk"), B(f"ss{h}")])
                            S.op("act", lambda e, h=h: e.activation(out=sc[:, 4 + h:5 + h], in_=sc[:, h:h + 1], func=AF.Ln, bias=epsc[:, 1:2], scale=1.0 / 1024.0),
                                 [B(f"ss{h}"), Bc], [B(f"rs{h}")])
                            S.op("act", lambda e, h=h: e.activation(out=sc[:, 4 + h:5 + h], in_=sc[:, 4 + h:5 + h], func=AF.Exp, scale=-0.5),
                                 [], [B(f"rs{h}")])
                            S.op("act", lambda e, oi=oi, h=h: e.activation(out=onb[:, h * 1024:(h + 1) * 1024], in_=osb[oi][:], func=AF.Copy, scale=sc[:, 4 + h:5 + h]),
                                 [Bo, B(f"rs{h}")], [B("onb")])
                    if has_next:
                        for c2 in range(2):
                            dc = 2 * h + c2
                            for half in range(2):
                                pu, pub = nps_h()
                                c0 = h * 1024 + half * 512
                                S.op("pe", lambda e, dc=dc, c0=c0, pu=pu: e.matmul(pu[:], lhsT=Kq[:, dc * 128:(dc + 1) * 128], rhs=vt[:, c0:c0 + 512], start=True, stop=True),
                                     [BKd, Bv], [pub])
                                S.op("dve", lambda e, dc=dc, half=half, pu=pu: e.scalar_tensor_tensor(out=Sst[:, dc, half * 512:(half + 1) * 512],
                                                                                                 in0=Sst[:, dc, half * 512:(half + 1) * 512], scalar=scq[:, 16 + dc:17 + dc],
                                                                                                 in1=pu[:], op0=ALU.mult, op1=ALU.add), [pub, Bsc], [B(f"Sst{dc}")])
                            S.op("act", lambda e, dc=dc: e.activation(out=Sbf[:, dc, :], in_=Sst[:, dc, :], func=AF.Copy), [B(f"Sst{dc}")], [B(f"Sbf{dc}")])
                if do_out and final:
                    Bz = B("zT_t")
                    S.dma(zT_t[:], zT_s[n], [B(f"fm{n}")], [Bz], Bz)
                    yi = alt(yT)
                    ByT = B(f"yT{yi}")
                    for j4 in range(4):
                        pt, pb_ = nps_h()
                        ptb = pt[:].bitcast(BF16)
                        for jj in range(8):
                            j = j4 * 8 + jj
                            S.op("pe", lambda e, j=j, jj=jj, ptb=ptb: e.transpose(out=ptb[:, jj * 128:(jj + 1) * 128], in_=onb[:, j * 128:(j + 1) * 128], identity=identb[:]),
                                 [B("onb"), Bc], [pb_], inc=(jj == 7))
                        S.op("dve", lambda e, j4=j4, ptb=ptb, yi=yi: e.tensor_tensor(out=yT[yi][:, j4 * 8:(j4 + 1) * 8, :].rearrange("p a c -> p (a c)"), in0=ptb[:, :],
                                                                                 in1=zT_t[:, j4 * 8:(j4 + 1) * 8, :].rearrange("p a c -> p (a c)"), op=ALU.mult), [pb_, Bz], [ByT])
                    S.dma(yT_s[n], yT[yi][:], [ByT], [B(f"yTs{n}")], ByT)

            def run_scan(chunks, d, out_pred, final):
                S.op("dve", lambda e: e.memset(Sst[:], 0.0), [], [B(f"Sst{dc}") for dc in range(8)])
                S.op("pool", lambda e: e.memset(Sbf[:], 0.0), [], [B(f"Sbf{dc}") for dc in range(8)])
                if not chunks:
                    return
                prep(chunks[0], d, out_pred(chunks[0]), 0)
                for i, n in enumerate(chunks):
                    if i + 1 < len(chunks):
                        prep(chunks[i + 1], d, out_pred(chunks[i + 1]), (i + 1) % 2)
                    heads(n, d, out_pred(n), final, i % 2)

            run_scan(list(range(0, T1 * 4)) if stage >= 3 else [], 0, lambda n: n >= T0 * 4, False)
            S.barrier()
            run_scan(list(range(NCH - 1, T0 * 4 - 1, -1)) if stage >= 4 else [], 1, lambda n: n < T1 * 4, True)
            S.barrier()

        with ExitStack() as es4:
            E4_ = es4.enter_context
            yTt2 = [E4_(nc.sbuf_tensor(f"yTt{i}", [128, 4, 32, 128], BF16)) for i in range(1)]
            rr4 = E4_(nc.sbuf_tensor("rr4", [128, 4, D], F32))
            wo = [E4_(nc.sbuf_tensor(f"wo{i}", [128, 16, 512], BF16)) for i in range(5)]
            bcg = [E4_(nc.sbuf_tensor(f"bcg{i}", [128, D], F32)) for i in range(4)]
            st4 = E4_(nc.sbuf_tensor("st4", [128, 4, 4, 6], F32))
            mv4 = E4_(nc.sbuf_tensor("mv4", [128, 4, 2], F32))
            Bbc = B("bcg")
            S.dma(bcg[0][:], ln_g[0:1, :].broadcast_to([128, D]), [], [Bbc], Bbc)
            S.dma(bcg[1][:], ln_b[0:1, :].broadcast_to([128, D]), [], [Bbc], Bbc)
            S.dma(bcg[2][:], ln_g[1:2, :].broadcast_to([128, D]), [], [Bbc], Bbc)
            S.dma(bcg[3][:], ln_b[1:2, :].broadcast_to([128, D]), [], [Bbc], Bbc)
            wo_k = {"issued": 0, "used": 0}
            tiles4 = list(range(T0, T1)) if stage >= 5 else []
            tot4 = len(tiles4) * 8

            def wo_issue():
                k = wo_k["issued"]
                if k >= tot4:
                    return
                s_ = k % 5
                Bw = B(f"wo{s_}")
                S.dma(wo[s_][:].rearrange("p k c -> p (k c)"), wB_out_s[k % 8], [B("wscr")], [Bw], Bw)
                wo_k["issued"] = k + 1

            for _ in range(4):
                wo_issue()
            for ti4, i in enumerate(tiles4):
                yTt = yTt2[0]
                for b in range(4):
                    n = i * 4 + b
                    By = B(f"yTt0_{b}")
                    S.dma(yTt[:, b, :, :], yT_s[n], [B(f"yTs{n}")], [By], By)
                    Br = B(f"rr4{b}")
                    S.dma(rr4[:, b, :], x1h_s[n * 128:(n + 1) * 128, :], [B(f"x1h{n}")], [Br], Br)
                    S.op("pool", lambda e, b=b: e.tensor_tensor(out=rr4[:, b, :], in0=rr4[:, b, :], in1=bcg[0][:], op=ALU.mult), [Bbc], [Br])
                    S.op("pool", lambda e, b=b: e.tensor_tensor(out=rr4[:, b, :], in0=rr4[:, b, :], in1=bcg[1][:], op=ALU.add), [Bbc], [Br])
                for nt in range(4):
                    pts = [nps() for _ in range(4)]
                    for hf in range(2):
                        k = wo_k["used"]
                        while wo_k["issued"] <= k:
                            wo_issue()
                        wo_k["used"] = k + 1
                        s_ = k % 5
                        Bw = B(f"wo{s_}")
                        for b in range(4):
                            pt, pb_ = pts[b]
                            for jj in range(16):
                                j = hf * 16 + jj
                                S.op("pe", lambda e, b=b, j=j, jj=jj, s_=s_, pt=pt, hf=hf, yTt=yTt: e.matmul(pt[:], lhsT=yTt[:, b, j, :], rhs=wo[s_][:, jj, :],
                                                                                            start=(hf == 0 and jj == 0), stop=(hf == 1 and jj == 15)),
                                     [B(f"yTt0_{b}"), Bw], [pb_], inc=(jj == 15))
                        wo_issue()
                    for b in range(4):
                        pt, pb_ = pts[b]
                        Br = B(f"rr4{b}")
                        S.op("dve", lambda e, b=b, nt=nt, pt=pt: e.scalar_tensor_tensor(out=rr4[:, b, nt * 512:(nt + 1) * 512], in0=rr4[:, b, nt * 512:(nt + 1) * 512],
                                                                                   scalar=ALPHA, in1=pt[:], op0=ALU.mult, op1=ALU.add), [pb_], [Br])
                        S.op("dve", lambda e, b=b, nt=nt: e.bn_stats(out=st4[:, b, nt, :], in_=rr4[:, b, nt * 512:(nt + 1) * 512]), [Br], [B(f"st4{b}")])
                for b in range(4):
                    n = i * 4 + b
                    Br = B(f"rr4{b}")
                    Bm = B(f"mv4{b}")
                    S.op("dve", lambda e, b=b: e.bn_aggr(out=mv4[:, b, :], in_=st4[:, b, :, :].rearrange("p a c -> p (a c)")), [B(f"st4{b}")], [Bm])
                    S.op("act", lambda e, b=b: e.activation(out=mv4[:, b, 1:2], in_=mv4[:, b, 1:2], func=AF.Ln, bias=epsc[:, 0:1]), [Bc], [Bm])
                    S.op("act", lambda e, b=b: e.activation(out=mv4[:, b, 1:2], in_=mv4[:, b, 1:2], func=AF.Exp, scale=-0.5), [], [Bm])
                    S.op("dve", lambda e, b=b: e.tensor_scalar(out=rr4[:, b, :], in0=rr4[:, b, :], scalar1=mv4[:, b, 0:1], scalar2=mv4[:, b, 1:2], op0=ALU.subtract, op1=ALU.mult),
                         [Bm], [Br])
                    S.op("pool", lambda e, b=b: e.tensor_tensor(out=rr4[:, b, :], in0=rr4[:, b, :], in1=bcg[2][:], op=ALU.mult), [Bbc], [Br])
                    S.op("pool", lambda e, b=b: e.tensor_tensor(out=rr4[:, b, :], in0=rr4[:, b, :], in1=bcg[3][:], op=ALU.add), [Bbc], [Br])
                    o0 = n * 128 - T0 * 512
                    S.dma(yout[o0:o0 + 128, :], rr4[:, b, :], [Br], [B("yout")], Br)
            S.barrier()
        S.finish()
    return nc


def make_consts(NCH, keepf, keepb, colsT=None):
    c = np.zeros((128, 7 * 128 + 2 * NCH + 128 + 256), np.float32)
    s = np.arange(128)[:, None]
    t = np.arange(128)[None, :]
    c[:, 0:128] = np.eye(128, dtype=np.float32)
    c[:, 128:256] = np.where(s <= t, -1.0 / 16.0, 0.0)
    c[:, 256:384] = np.where(s > t, -1.0 / 16.0, 0.0)
    c[:, 384:512] = np.where(s >= t, -1.0 / 16.0, 0.0)
    c[:, 512:640] = np.where(s < t, -1.0 / 16.0, 0.0)
    c[:, 640:768] = np.where(s <= t, 1.0, 0.0)
    c[:, 768:896] = np.where(s > t, 1.0, 0.0)
    c[:, 896:896 + NCH] = keepf[None, :]
    c[:, 896 + NCH:896 + 2 * NCH] = keepb[None, :]
    if colsT is not None:
        c[:, 896 + 2 * NCH:896 + 2 * NCH + 128] = colsT
    o = 896 + 2 * NCH + 128
    c[:, o:o + 128] = c[:, 128:256] - c[:, 128 + 64:128 + 65]
    c[:, o + 128:o + 256] = c[:, 384:512] - c[:, 384 + 64:384 + 65]
    return c


def common_inputs(w_in_a, ln_v_g_a, ln_v_b_a, w_s_a, b_s_a, w_out_a, w_in_b, w_g2_b, b_g_b, gn_g_b, w_out_b, ln_g, ln_b):
    f = lambda a: np.ascontiguousarray(np.asarray(a, dtype=np.float32))
    return {
        "w_in_a": f(w_in_a[0]),
        "wsT": np.ascontiguousarray(f(w_s_a[0]).transpose(2, 0, 1).reshape(128, 2048)),
        "b_s_a": f(b_s_a[0]).reshape(1, 2048),
        "w_out_a": f(w_out_a[0]),
        "w_in_b": f(w_in_b[0]),
        "w_g2_b": f(w_g2_b[0]),
        "b_g_b": f(b_g_b[0]).reshape(1, 2048),
        "w_out_b": f(w_out_b[0]),
        "ln_g": f(ln_g),
        "ln_b": f(ln_b),
    }


def param_cols(ln_v_g_a, ln_v_b_a, gn_g_b, ln_g, ln_b):
    f = lambda a, n: np.asarray(a, np.float32).reshape(n, 128).T
    return np.ascontiguousarray(np.concatenate([f(ln_v_g_a[0], 32), f(ln_v_b_a[0], 32), f(gn_g_b[0], 32), f(ln_g[0], 16), f(ln_b[0], 16)], 1))


_NC_CACHE = {}


def kernel(x_prompt, x_sample, w_in_a, ln_v_g_a, ln_v_b_a, w_s_a, b_s_a, w_out_a,
           w_in_b, w_g2_b, b_g_b, gn_g_b, w_out_b, ln_g, ln_b):
    x_prompt = np.asarray(x_prompt, np.float32)
    x_sample = np.asarray(x_sample, np.float32)
    X = np.concatenate([x_prompt.reshape(-1, D), x_sample.reshape(-1, D)], 0)
    NTOK = X.shape[0]
    starts = {0, x_prompt.shape[0] * x_prompt.shape[1], x_prompt.shape[0] * x_prompt.shape[1] + x_sample.shape[1]}
    NT = (OWN + 2 * HALO) // 512
    NCH = NT * 4
    common = common_inputs(w_in_a, ln_v_g_a, ln_v_b_a, w_s_a, b_s_a, w_out_a, w_in_b, w_g2_b, b_g_b, gn_g_b, w_out_b, ln_g, ln_b)
    colsT = param_cols(ln_v_g_a, ln_v_b_a, gn_g_b, ln_g, ln_b)
    in_maps = []
    for c in range(NCORES):
        lo = c * OWN - HALO
        xw = np.zeros((NT * 512, D), np.float32)
        a, b = max(lo, 0), min(lo + NT * 512, NTOK)
        xw[a - lo:b - lo] = X[a:b]
        keepf = np.ones(NCH, np.float32)
        keepb = np.ones(NCH, np.float32)
        for n in range(NCH):
            g0 = lo + n * 128
            if g0 in starts or g0 <= 0 or g0 >= NTOK:
                keepf[n] = 0.0
            g1 = g0 + 128
            if g1 in starts or g1 <= 0 or g1 >= NTOK:
                keepb[n] = 0.0
        m = dict(common)
        m["xw"] = xw
        m["cst"] = make_consts(NCH, keepf, keepb, colsT)
        in_maps.append(m)
    key = (NT,)
    if key not in _NC_CACHE:
        _NC_CACHE[key] = build(NT, 1, NT - 1)
    nc = _NC_CACHE[key]
    res = run_bass_kernel_spmd(nc, in_maps, core_ids=list(range(NCORES)))
    Y = np.concatenate([np.asarray(res.results[c]["y"], np.float32) for c in range(NCORES)], 0)
    n_p = x_prompt.shape[0] * x_prompt.shape[1]
    y_prompt = Y[:n_p].reshape(x_prompt.shape)
    y_sample = Y[n_p:].reshape(x_sample.shape)
    return (y_prompt, y_sample)
```
